# Optimizing a Trainium2 kernel written in Bass

```python
import jax, jax.numpy as jnp
from jax import lax
import numpy as np

D_MODEL = 1024
BATCH = 4
SEQ = 8192
DEPTH = 4
DEC_BATCH = 8
DEC_SEQ = 32
PAST_LEN = 1024

CHUNK = 64
N_MEM = 256
CONV_W = 3
CONV_DIM = 512
RET_HEADS = 4
RET_HD = 128
RET_DIM = RET_HEADS * RET_HD
XA_HEADS = 4
XA_HD = 128
XA_DIM = XA_HEADS * XA_HD
D_MIX = CONV_DIM + RET_DIM + XA_DIM
SPLIT_SIZES = (CONV_DIM, CONV_DIM, CONV_DIM, CONV_DIM, RET_DIM, RET_DIM, RET_DIM, RET_DIM, XA_DIM, XA_DIM)
D_IN = sum(SPLIT_SIZES)
SPLIT_IDX = tuple(int(i) for i in np.cumsum(SPLIT_SIZES)[:-1])
ROPE_BASE = 10000.0
EPS = 1e-6

kernel_name = "hybrid_shortconv_retention_memxattn_step"


def rmsnorm(x, g):
    xf = x.astype(jnp.float32)
    y = xf * lax.rsqrt(jnp.mean(xf * xf, axis=-1, keepdims=True) + EPS) * g.astype(jnp.float32)
    return y.astype(x.dtype)


def head_rms(o):
    return o * lax.rsqrt(jnp.mean(o * o, axis=-1, keepdims=True) + EPS)


def rope(x, pos):
    d = x.shape[-1]
    freqs = ROPE_BASE ** (-jnp.arange(0, d, 2, dtype=jnp.float32) / d)
    ang = pos[:, None] * freqs[None, :]
    c = jnp.cos(ang)[None, :, None, :]
    s = jnp.sin(ang)[None, :, None, :]
    x1, x2 = x[..., : d // 2], x[..., d // 2:]
    return jnp.concatenate([x1 * c - x2 * s, x1 * s + x2 * c], axis=-1)


def log_gamma():
    return jnp.log(1.0 - jnp.power(2.0, -5.0 - jnp.arange(RET_HEADS, dtype=jnp.float32)))


def decay_mask(n, lg):
    idx = jnp.arange(n, dtype=jnp.float32)
    diff = idx[:, None] - idx[None, :]
    causal = diff >= 0
    return jnp.where(causal[None], jnp.exp(jnp.where(causal, diff, 0.0)[None] * lg[:, None, None]), 0.0)


def retention_prompt(q, k, v, lg):
    Bn, S, H, d = q.shape
    nC = S // CHUNK
    qc = q.reshape(Bn, nC, CHUNK, H, d)
    kc = k.reshape(Bn, nC, CHUNK, H, d)
    vc = v.reshape(Bn, nC, CHUNK, H, d)
    idx = jnp.arange(CHUNK, dtype=jnp.float32)
    dmat = decay_mask(CHUNK, lg)
    scores = jnp.einsum('bnihd,bnjhd->bnhij', qc, kc) * dmat[None, None]
    o_intra = jnp.einsum('bnhij,bnjhd->bnihd', scores, vc)
    k_dec = jnp.exp((CHUNK - 1 - idx)[:, None] * lg[None, :])
    kv = jnp.einsum('bnjhd,bnjhe->bnhde', kc * k_dec[None, None, :, :, None], vc)
    q_dec = jnp.exp((idx + 1)[:, None] * lg[None, :])
    chunk_dec = jnp.exp(CHUNK * lg)

    def step(s_prev, inp):
        q_n, kv_n = inp
        o = jnp.einsum('bihd,bhde->bihe', q_n * q_dec[None, :, :, None], s_prev)
        s_new = s_prev * chunk_dec[None, :, None, None] + kv_n
        return s_new, o

    s0 = jnp.zeros((Bn, H, d, d), jnp.float32)
    s_fin, o_inter = lax.scan(step, s0, (jnp.moveaxis(qc, 1, 0), jnp.moveaxis(kv, 1, 0)))
    o = o_intra + jnp.moveaxis(o_inter, 0, 1)
    return o.reshape(Bn, S, H, d), s_fin


def retention_sample(q, k, v, s_prev, lg):
    L = q.shape[1]
    idx = jnp.arange(L, dtype=jnp.float32)
    dmat = decay_mask(L, lg)
    scores = jnp.einsum('bihd,bjhd->bhij', q, k) * dmat[None]
    q_dec = jnp.exp((idx + 1)[:, None] * lg[None, :])
    k_dec = jnp.exp((L - 1 - idx)[:, None] * lg[None, :])
    o = (jnp.einsum('bhij,bjhd->bihd', scores, v)
         + jnp.einsum('bihd,bhde->bihe', q * q_dec[None, :, :, None], s_prev))
    s_new = (s_prev * jnp.exp(L * lg)[None, :, None, None]
             + jnp.einsum('bjhd,bjhe->bhde', k * k_dec[None, :, :, None], v))
    return o, s_new


def mem_kv(mem, g, w):
    m = rmsnorm(mem, g) @ w
    Bn = mem.shape[0]
    mk = m[..., :XA_DIM].reshape(Bn, N_MEM, XA_HEADS, XA_HD)
    mv = m[..., XA_DIM:].reshape(Bn, N_MEM, XA_HEADS, XA_HD)
    return mk, mv


def mem_attend(q, mk, mv):
    s = jnp.einsum('blhd,bmhd->bhlm', q, mk).astype(jnp.float32) * (XA_HD ** -0.5)
    p = jax.nn.softmax(s, axis=-1)
    return jnp.einsum('bhlm,bmhd->blhd', p.astype(mv.dtype), mv)


def layer(x, conv_state, ret_state, mk, mv, pos, norm_g, w_in, conv_w, w_out):
    Bn, L, _ = x.shape
    h = rmsnorm(x, norm_g)
    p = h @ w_in
    b_c, c_c, h_c, z_c, q_r, k_r, v_r, z_r, q_x, z_x = jnp.split(p, SPLIT_IDX, axis=-1)

    u = c_c * h_c
    u_ext = jnp.concatenate([conv_state.astype(u.dtype), u], axis=1)
    conv = sum(conv_w[j] * u_ext[:, j:j + L] for j in range(CONV_W))
    y_c = b_c * conv * jax.nn.silu(z_c)
    new_conv = u_ext[:, -(CONV_W - 1):]

    lg = log_gamma()
    q = rope(q_r.reshape(Bn, L, RET_HEADS, RET_HD).astype(jnp.float32), pos)
    k = rope(k_r.reshape(Bn, L, RET_HEADS, RET_HD).astype(jnp.float32), pos) * (RET_HD ** -0.5)
    v = v_r.reshape(Bn, L, RET_HEADS, RET_HD).astype(jnp.float32)
    if ret_state is None:
        o, new_ret = retention_prompt(q, k, v, lg)
    else:
        o, new_ret = retention_sample(q, k, v, ret_state.astype(jnp.float32), lg)
    y_r = head_rms(o).reshape(Bn, L, RET_DIM).astype(x.dtype) * jax.nn.silu(z_r)

    ox = mem_attend(q_x.reshape(Bn, L, XA_HEADS, XA_HD), mk, mv)
    y_x = ox.reshape(Bn, L, XA_DIM) * jax.nn.silu(z_x)

    y = jnp.concatenate([y_c, y_r, y_x], axis=-1) @ w_out
    return x + y, new_conv, new_ret


def setup_inputs(seed: int = 0) -> dict:
    key = jax.random.key(seed)
    ks = jax.random.split(key, 16)
    f32 = jnp.float32
    return {
        "x_prompt": jax.random.normal(ks[0], (BATCH, SEQ, D_MODEL), f32),
        "x_sample": jax.random.normal(ks[1], (DEC_BATCH, DEC_SEQ, D_MODEL), f32),
        "mem_prompt": jax.random.normal(ks[2], (BATCH, N_MEM, D_MODEL), f32),
        "state_conv": jax.random.normal(ks[3], (DEPTH, DEC_BATCH, CONV_W - 1, CONV_DIM), f32),
        "state_ret": 0.5 * jax.random.normal(ks[4], (DEPTH, DEC_BATCH, RET_HEADS, RET_HD, RET_HD), f32),
        "cache_mem_k": jax.random.normal(ks[5], (DEPTH, DEC_BATCH, N_MEM, XA_HEADS, XA_HD), f32),
        "cache_mem_v": jax.random.normal(ks[6], (DEPTH, DEC_BATCH, N_MEM, XA_HEADS, XA_HD), f32),
        "norm_g": 1.0 + 0.02 * jax.random.normal(ks[7], (DEPTH, D_MODEL), f32),
        "w_in": jax.random.normal(ks[8], (DEPTH, D_MODEL, D_IN), f32) * D_MODEL ** -0.5,
        "conv_w": jax.random.normal(ks[9], (DEPTH, CONV_W, CONV_DIM), f32) * CONV_W ** -0.5,
        "mem_norm_g": 1.0 + 0.02 * jax.random.normal(ks[10], (DEPTH, D_MODEL), f32),
        "w_mem_kv": jax.random.normal(ks[11], (DEPTH, D_MODEL, 2 * XA_DIM), f32) * D_MODEL ** -0.5,
        "w_out": jax.random.normal(ks[12], (DEPTH, D_MIX, D_MODEL), f32) * D_MIX ** -0.5,
        "final_norm_g": 1.0 + 0.02 * jax.random.normal(ks[13], (D_MODEL,), f32),
    }


def reference(x_prompt, x_sample, mem_prompt, state_conv, state_ret, cache_mem_k, cache_mem_v,
              norm_g, w_in, conv_w, mem_norm_g, w_mem_kv, w_out, final_norm_g):
    pos_p = jnp.arange(x_prompt.shape[1], dtype=jnp.float32)
    pos_s = PAST_LEN + jnp.arange(x_sample.shape[1], dtype=jnp.float32)
    xp, xs = x_prompt, x_sample
    conv_p, ret_p, mk_p, mv_p, conv_s, ret_s = [], [], [], [], [], []
    for l in range(DEPTH):
        mk, mv = mem_kv(mem_prompt, mem_norm_g[l], w_mem_kv[l])
        zero_conv = jnp.zeros((xp.shape[0], CONV_W - 1, CONV_DIM), xp.dtype)
        xp, c_new, r_new = layer(xp, zero_conv, None, mk, mv, pos_p,
                                 norm_g[l], w_in[l], conv_w[l], w_out[l])
        conv_p.append(c_new); ret_p.append(r_new); mk_p.append(mk); mv_p.append(mv)
        xs, c_new, r_new = layer(xs, state_conv[l], state_ret[l], cache_mem_k[l], cache_mem_v[l], pos_s,
                                 norm_g[l], w_in[l], conv_w[l], w_out[l])
        conv_s.append(c_new); ret_s.append(r_new)
    y_prompt = rmsnorm(xp, final_norm_g)
    y_sample = rmsnorm(xs, final_norm_g)
    return (y_prompt, y_sample,
            jnp.stack(conv_p), jnp.stack(ret_p), jnp.stack(mk_p), jnp.stack(mv_p),
            jnp.stack(conv_s), jnp.stack(ret_s))
```

```python
import os
import numpy as np
from contextlib import ExitStack
import concourse.bass as bass
import concourse.mybir as mybir
from concourse.bass_utils import run_bass_kernel_spmd

F32 = mybir.dt.float32
BF16 = mybir.dt.bfloat16
AF = mybir.ActivationFunctionType
ALU = mybir.AluOpType

D = 1024
DEPTH = 4
SEQ = 8192
NMEM = 256
H = 4
HD = 128
EPS = 1e-6
PAST = 1024
TP = 512
NCORES = 8


class _Op:
    __slots__ = ("eng", "fn", "waits", "sem", "count", "is_dma")


class Sched:
    ENGS = ("pe", "act", "dve", "pool", "sp")

    def __init__(self, nc):
        self.nc = nc
        self.ops = {e: [] for e in self.ENGS}
        self.count = {e: 0 for e in self.ENGS}
        self.last_writer = {}
        self.readers = {}
        self.seen = {e: {} for e in self.ENGS}
        self.dma_count = {}
        self.sem_handles = {}

    def _need(self, eng, tok, waits, same_ok):
        semkey, cnt, src_eng = tok
        if src_eng == eng and same_ok:
            return
        if self.seen[eng].get(semkey, 0) >= cnt:
            return
        self.seen[eng][semkey] = cnt
        for i, (sk, c) in enumerate(waits):
            if sk == semkey:
                waits[i] = (sk, max(c, cnt))
                return
        waits.append((semkey, cnt))

    def op(self, eng, fn, reads=(), writes=(), dma=None):
        o = _Op()
        o.eng = eng
        o.fn = fn
        o.waits = []
        o.is_dma = dma is not None
        same_ok = (eng == "pe") and not o.is_dma
        for k in reads:
            w = self.last_writer.get(k)
            if w is not None:
                self._need(eng, w, o.waits, same_ok)
        for k in writes:
            w = self.last_writer.get(k)
            if w is not None:
                self._need(eng, w, o.waits, same_ok)
            for r in self.readers.get(k, ()):
                self._need(eng, r, o.waits, same_ok)
        if o.is_dma:
            semkey = ("dma", eng, dma)
            self.dma_count[semkey] = self.dma_count.get(semkey, 0) + 1
            cnt = 16 * self.dma_count[semkey]
            src = None
        else:
            semkey = eng
            self.count[eng] += 1
            cnt = self.count[eng]
            src = eng
        o.sem = semkey
        o.count = cnt
        tok = (semkey, cnt, src)
        for k in reads:
            self.readers.setdefault(k, []).append(tok)
        for k in writes:
            self.last_writer[k] = tok
            self.readers[k] = []
        self.ops[eng].append(o)
        return tok

    def final_wait(self, eng, toks):
        o = _Op()
        o.eng = eng
        o.fn = None
        o.waits = []
        o.is_dma = False
        o.sem = None
        o.count = 0
        for t in toks:
            self._need(eng, t, o.waits, False)
        self.ops[eng].append(o)

    def emit(self, stack):
        nc = self.nc
        semkeys = set()
        for e in self.ENGS:
            for o in self.ops[e]:
                if o.sem is not None:
                    semkeys.add(o.sem)
                for sk, _ in o.waits:
                    semkeys.add(sk)
        for i, sk in enumerate(sorted(semkeys, key=str)):
            self.sem_handles[sk] = stack.enter_context(nc.semaphore("s%d" % i))
        block = stack.enter_context(nc.Block())
        sems = self.sem_handles

        def make(engkey):
            ops = self.ops[engkey]

            def body(e):
                for o in ops:
                    for sk, c in o.waits:
                        e.wait_ge(sems[sk], c)
                    if o.fn is None:
                        continue
                    ins = o.fn(e)
                    ins.then_inc(sems[o.sem], 16 if o.is_dma else 1)
            return body

        block.sync(make("sp"))
        block.scalar(make("act"))
        block.vector(make("dve"))
        block.gpsimd(make("pool"))
        block.tensor(make("pe"))


class Rot:
    def __init__(self, name, n):
        self.name, self.n, self.i = name, n, 0

    def next(self):
        i = self.i % self.n
        self.i += 1
        return i, (self.name, i)


def build_program(n_ptiles):
    nc = bass.Bass("TRN2", target_bir_lowering=False)
    st = ExitStack()

    def din(name, shape, dt=F32):
        return nc.dram_tensor(name, list(shape), dt, kind="ExternalInput").ap()

    def dout(name, shape, dt=F32):
        return nc.dram_tensor(name, list(shape), dt, kind="ExternalOutput").ap()

    xp_d = din("xp", [D, SEQ])
    xs_d = din("xs", [D, 32])
    mem_d = din("mem", [D, NMEM])
    sconv_d = din("sconv", [128, DEPTH, 4, 2])
    sret_d = din("sret", [DEPTH, H, HD, HD])
    cmk_d = din("cmk", [DEPTH, NMEM, 512])
    cmv_d = din("cmv", [DEPTH, NMEM, 512])
    w_in_d = din("w_in", [DEPTH, D, 5120])
    w_kv_d = din("w_kv", [DEPTH, D, 1024])
    w_out_d = din("w_out", [DEPTH, 1536, D])
    g_d = din("g_all", [128, DEPTH, 8])
    gmem_d = din("gmem", [128, DEPTH, 8])
    gfin_d = din("gfin", [128, 8])
    convw_d = din("convw", [128, DEPTH, 3, 4])
    ident_d = din("ident", [128, 128])
    mask_d = din("mask", [128, 128])
    dqb_d = din("dqb", [128, H, 128])
    dkb_d = din("dkb", [128, H, 128])
    dkcol_d = din("dkcol", [128, H])
    gp_d = din("gtab_p", [128, H, 128])
    gs_d = din("gtab_s", [128, H, 128])
    c2p_d = din("c2p", [128, SEQ // 128, 128])
    s2p_d = din("s2p", [128, SEQ // 128, 128])
    c2s_d = din("c2s", [32, 1, 128])
    s2s_d = din("s2s", [32, 1, 128])

    yp_d = dout("yp", [D, SEQ])
    ys_d = dout("ys", [D, 32])
    ncp_d = dout("ncp", [128, DEPTH, 4, 2])
    nrp_d = dout("nrp", [DEPTH, H, HD, HD])
    nmk_d = dout("nmk", [DEPTH, NMEM, 512])
    nmv_d = dout("nmv", [DEPTH, NMEM, 512])
    ncs_d = dout("ncs", [128, DEPTH, 4, 2])
    nrs_d = dout("nrs", [DEPTH, H, HD, HD])

    winb_d = nc.dram_tensor("winb", [DEPTH, 10, 128, 8, 512], BF16).ap()
    woutb_d = nc.dram_tensor("woutb", [DEPTH, 4, 128, 12, 256], BF16).ap()
    wkvb_d = nc.dram_tensor("wkvb", [DEPTH, 2, 128, 8, 512], BF16).ap()

    with st:
        def sb(name, shape, dt=F32):
            return st.enter_context(nc.sbuf_tensor("sb_" + name, list(shape), dt))

        xT = sb("xT", [128, 8, TP])
        xn = sb("xn", [128, 8, TP], BF16)
        xsq = sb("xsq", [128, 2, TP], BF16)
        Aq = sb("Aq", [128, 4, 512], BF16)
        Btmp = sb("Btmp", [128, 2, 512], BF16)
        rsb = sb("rsb", [128, 2, TP])
        xg = sb("xg", [128, 8, TP], BF16)
        rs1 = sb("rs1", [1, TP])
        rtok = sb("rtok", [128, 4])
        Ak = sb("Ak", [128, 4, 512], BF16)
        v_sb = sb("v_sb", [128, 4, 512], BF16)
        vdec = sb("vdec", [128, 4, 512], BF16)
        qT = sb("qT", [128, H, TP], BF16)
        kT = sb("kT", [128, H, TP], BF16)
        qxT = sb("qxT", [128, H, TP], BF16)
        PT = sb("PT", [128, 2, 2, TP], BF16)
        scrA = sb("scrA", [128, 2048])
        orn = sb("orn", [128, H, TP])
        osq = sb("osq", [128, 4, TP], BF16)
        scm = sb("scm", [128, 2, H, 128], BF16)
        Sbf = sb("Sbf", [128, 2, H, 128], BF16)
        Sst = sb("Sst", [128, DEPTH, H, 128])
        utail = sb("utail", [128, DEPTH, 4, 2])
        c_sb = sb("c_sb", [128, TP])
        ubuf = sb("ubuf", [128, 2, TP + 2])
        s0b = sb("s0b", [128, 2, TP])
        s1b = sb("s1b", [128, TP])
        s2b = sb("s2b", [128, TP])
        szb = sb("szb", [128, 2, TP])
        yT = sb("yT", [128, 12, TP], BF16)
        wring = sb("wring", [128, 3, 4096], BF16)
        KT = sb("KT", [128, DEPTH, H, NMEM], BF16)
        Vm = sb("Vm", [128, DEPTH, 2, 512], BF16)
        xstage = sb("xstage", [128, 2, 1024])
        C2 = sb("C2", [128, 4, 128])
        S2 = sb("S2", [128, 4, 128])
        ident_f = sb("ident_f", [128, 128])
        ident_b = sb("ident_b", [128, 128], BF16)
        ones_rms = sb("ones_rms", [128, 128], BF16)
        ones_h = sb("ones_h", [128, 128], BF16)
        ones_1 = sb("ones_1", [128, 128], BF16)
        mask_f = sb("mask_f", [128, 128])
        mask_b = sb("mask_b", [128, 128], BF16)
        dqb = sb("dqb", [128, H, 128])
        dkb = sb("dkb", [128, H, 128])
        dkcol = sb("dkcol", [128, H])
        gtab_p = sb("gtab_p", [128, H, 128])
        gtab_s = sb("gtab_s", [128, H, 128])
        g_all = sb("g_all", [128, DEPTH, 8])
        gmem = sb("gmem", [128, DEPTH, 8])
        gfin = sb("gfin", [128, 8])
        convw = sb("convw", [128, DEPTH, 3, 4])
        memn = sb("memn", [128, 8, NMEM], BF16)
        kvstage = sb("kvstage", [128, 1, 512])
        Kb = sb("Kb", [128, 2, 512], BF16)

        psb = [st.enter_context(nc.psum_tensor("ps%d" % i, [128, 512], F32)) for i in range(8)]

        S = Sched(nc)
        held = set()
        bank_ctr = [0]

        def bank(hold=False):
            while True:
                i = bank_ctr[0] % 8
                bank_ctr[0] += 1
                if i not in held:
                    break
            if hold:
                held.add(i)
            return i, ("ps", i)

        def release(i):
            held.discard(i)

        rot_xsq = Rot("xsq", 2)
        rot_pt = Rot("PT", 2)
        rot_osq = Rot("osq", 4)
        rot_scm = Rot("scm", 2)
        rot_sbf = Rot("Sbf", 2)
        rot_u = Rot("ubuf", 2)
        rot_s0 = Rot("s0b", 2)
        rot_sz = Rot("szb", 2)
        rot_bt = Rot("Btmp", 2)
        rot_rs = Rot("rsb", 2)
        rot_xst = Rot("xstage", 2)
        rot_kvs = Rot("kvstage", 1)

        def xkeys(i):
            return [(("xstage", i), 0), (("xstage", i), 1)]

        def dma(q, out, in_, reads, writes, slot, **kw):
            return S.op(q, lambda e: e.dma_start(out=out, in_=in_, **kw), reads, writes, dma=slot)

        wseq = []

        def layer_wseq(l):
            return [("in", l, 4), ("in", l, 5), ("in", l, 6), ("in", l, 8),
                    ("in", l, 0), ("in", l, 1), ("in", l, 2), ("in", l, 3),
                    ("in", l, 9), ("in", l, 7),
                    ("out", l, 0), ("out", l, 1), ("out", l, 2), ("out", l, 3)]

        if n_ptiles > 0:
            for l in range(DEPTH):
                wseq += [("kv", l, 0), ("kv", l, 1)]
        for _t in range(n_ptiles):
            for l in range(DEPTH):
                wseq += layer_wseq(l)
        for l in range(DEPTH):
            wseq += layer_wseq(l)
        NW = 3
        WQ2 = os.environ.get("MK_WQ2", "0") == "1"
        wstate = {"issued": 0, "next": 0}

        def load_const(t, d, key):
            dma("sp", t[:], d, [], [key], key)

        load_const(ident_f, ident_d, "ident_f")
        load_const(mask_f, mask_d, "mask_f")
        load_const(dqb, dqb_d, "dqb")
        load_const(dkb, dkb_d, "dkb")
        load_const(dkcol, dkcol_d, "dkcol")
        load_const(gtab_p, gp_d, "gtab_p")
        load_const(gtab_s, gs_d, "gtab_s")
        load_const(g_all, g_d, "g_all")
        load_const(gmem, gmem_d, "gmem")
        load_const(gfin, gfin_d, "gfin")
        load_const(convw, convw_d, "convw")
        S.op("dve", lambda e: e.tensor_copy(out=ident_b[:], in_=ident_f[:]), ["ident_f"], ["ident_b"])
        S.op("dve", lambda e: e.tensor_copy(out=mask_b[:], in_=mask_f[:]), ["mask_f"], ["mask_b"])
        S.op("dve", lambda e: e.memset(ones_rms[:], 1.0 / D), [], ["ones_rms"])
        S.op("dve", lambda e: e.memset(ones_h[:], 1.0 / HD), [], ["ones_h"])
        S.op("dve", lambda e: e.memset(ones_1[:], 1.0), [], ["ones_1"])

        pc_ctr = [0]
        pc_last = {}
        NPC = int(os.environ.get("MK_NPC", "4"))

        def pc_dma(dst, src_ap, key):
            i = pc_ctr[0] % NPC
            pc_ctr[0] += 1
            slot = ("precast", i)
            tok = dma("pool", dst, src_ap, [("pcslot", i)], [key, ("pcslot", i)], slot)
            return tok

        def precast_layer(l):
            keys = []

            def in_block(blk, c0):
                keys.append(("winb", l, blk))
                return pc_dma(winb_d[l, blk], w_in_d[l][:, c0:c0 + 512].rearrange("(kc p) c -> p kc c", p=128),
                              ("winb", l, blk))
            for blk, c0 in ((4, 2048), (5, 2560), (6, 3072), (8, 4096)):
                in_block(blk, c0)
            for cc in range(4):
                for s_i in range(4):
                    c0 = s_i * 512 + cc * 128
                    keys.append(("winb", l, cc, s_i))
                    pc_dma(winb_d[l, cc][:, :, s_i * 128:(s_i + 1) * 128],
                           w_in_d[l][:, c0:c0 + 128].rearrange("(kc p) c -> p kc c", p=128), ("winb", l, cc, s_i))
            for blk, c0 in ((9, 4608), (7, 3584)):
                in_block(blk, c0)
            tok = None
            for j in range(4):
                keys.append(("woutb", l, j))
                tok = pc_dma(woutb_d[l, j], w_out_d[l][:, j * 256:(j + 1) * 256].rearrange("(kc p) c -> p kc c", p=128),
                             ("woutb", l, j))

        def precast_kv(l):
            tok = None
            for j in range(2):
                tok = pc_dma(wkvb_d[l, j], w_kv_d[l][:, j * 512:(j + 1) * 512].rearrange("(kc p) c -> p kc c", p=128),
                             ("wkvb", l, j))

        if n_ptiles > 0:
            for l in range(DEPTH):
                precast_kv(l)
        for l in range(DEPTH):
            precast_layer(l)

        def w_read_keys(kind, l, blk):
            if kind == "in" and blk < 4:
                return [("winb", l, blk, s_i) for s_i in range(4)]
            return [({"in": "winb", "kv": "wkvb", "out": "woutb"}[kind], l, blk)]

        def w_issue_upto2(i):
            while wstate["issued"] <= min(i, len(wseq) - 1):
                j = wstate["issued"]
                kind, l, blk = wseq[j]
                slot = j % NW
                if kind == "in":
                    src = winb_d[l, blk]
                    dst = wring[:, slot, :].rearrange("p (k c) -> p k c", k=8)
                elif kind == "kv":
                    src = wkvb_d[l, blk] if os.environ.get("MK_KVSRC") != "1" else winb_d[l, 4 + blk]
                    dst = wring[:, slot, :].rearrange("p (k c) -> p k c", k=8)
                else:
                    src = woutb_d[l, blk]
                    dst = wring[:, slot, 0:3072].rearrange("p (k c) -> p k c", k=12)
                wq = "act" if (WQ2 and j >= len(wseq) - DEPTH * 14 and j % 2 == 1) else "sp"
                dma(wq, dst, src, w_read_keys(kind, l, blk), [("wring", slot)], ("wring", slot))
                wstate["issued"] += 1

        def w_next2(kind, l, blk):
            j = wstate["next"]
            while wseq[j] != (kind, l, blk):
                assert wstate["issued"] <= j
                wseq.pop(j)
            w_issue_upto2(j + NW - 1)
            wstate["next"] += 1
            slot = j % NW
            if kind == "out":
                view = wring[:, slot, 0:3072].rearrange("p (k c) -> p k c", k=12)
            else:
                view = wring[:, slot, :].rearrange("p (k c) -> p k c", k=8)
            return view, ("wring", slot)
        w_next = w_next2

        def rms_stats(src_fn, src_keys, nkc, T, ones_t, ones_key, eps):
            bi, bk = bank(hold=True)
            ps = psb[bi]
            for kc in range(nkc):
                si, sk = rot_xsq.next()
                S.op("act", lambda e, kc=kc, si=si: e.activation(out=xsq[:, si, :T], in_=src_fn(kc), func=AF.Square),
                     [src_keys(kc)], [sk])
                S.op("pe", lambda e, kc=kc, si=si: e.matmul(out=ps[:, :T], lhsT=ones_t[:], rhs=xsq[:, si, :T],
                                                           start=(kc == 0), stop=(kc == nkc - 1)),
                     [sk, ones_key], [bk])
            S.op("act", lambda e: e.activation(out=ps[:, :T], in_=ps[:, :T], func=AF.Ln, bias=eps), [bk], [bk])
            S.op("act", lambda e: e.activation(out=ps[:, :T], in_=ps[:, :T], func=AF.Exp, scale=-0.5), [bk], [bk])
            return bi, bk

        DEFERQ = os.environ.get("MK_DEFERQ", "1") == "1"
        PEng = ["pool"]

        def sigmoid_gate(pz, pzk, T):
            szi, szk = rot_sz.next()
            S.op("act", lambda e: e.activation(out=szb[:, szi, :T], in_=psb[pz][:, :T], func=AF.Silu), [pzk], [szk])
            return szi, szk

        def stage_view(kc):
            if kc < 4:
                return (xstage[:].rearrange("p a c -> p (a c)").rearrange("p (k t) -> p k t", k=4)[:, kc, :],
                        xkeys(0) + xkeys(1))
            return (xg[:].rearrange("p k t -> p (k t)").bitcast(F32).rearrange("p (k t) -> p k t", k=4)[:, kc - 4, :],
                    [("xg", j) for j in range(8)])

        def tile_layer(l, T, TB, gtab, gtab_key, stats=None, staged=False, prefetch=None, after_norm=None):
            NB = T // TB
            g_key = "g_all"
            skey = ("S", l)
            sbi, sbk = rot_sbf.next()
            S.op(PEng[0], lambda e, sbi=sbi: e.tensor_copy(out=Sbf[:, sbi], in_=Sst[:, l]), [skey], [sbk])
            defer_q = False
            if stats is None:
                bi, bk = rms_stats(lambda kc: xT[:, kc, :T], lambda kc: ("xT", kc), 8, T, ones_rms, "ones_rms", EPS)
            else:
                bi, bk, defer_q = stats
            for kc in range(8):
                if staged:
                    sv, skeys = stage_view(kc)
                else:
                    sv, skeys = xT[:, kc, :T], [("xT", kc)]
                S.op("dve", lambda e, kc=kc, bi=bi, sv=sv: e.scalar_tensor_tensor(
                    out=xn[:, kc, :T], in0=sv, scalar=g_all[:, l, kc:kc + 1], in1=psb[bi][:, :T],
                    op0=ALU.mult, op1=ALU.mult), skeys + [bk, g_key], [("xn", kc)])
            release(bi)
            if after_norm is not None:
                after_norm()
            if staged:
                for kc in range(8):
                    sv, skeys = stage_view(kc)
                    if kc < 4:
                        S.op("act", lambda e, kc=kc, sv=sv: e.activation(out=xT[:, kc, :T], in_=sv, func=AF.Copy),
                             skeys, [("xT", kc)])
                    else:
                        S.op(PEng[0], lambda e, kc=kc, sv=sv: e.tensor_copy(out=xT[:, kc, :T], in_=sv), skeys, [("xT", kc)])

            for name, wi in (("q", 4), ("k", 5), ("v", 6)):
                wv, wk = w_next("in", l, wi)
                pbanks = [bank() for _ in range(NB)]
                dq_ = defer_q and name == "q"
                src_t, src_n = (xg, "xg") if dq_ else (xn, "xn")
                for kc in range(8):
                    for blk in range(NB):
                        pi, pk = pbanks[blk]
                        S.op("pe", lambda e, kc=kc, blk=blk, pi=pi, wv=wv, src_t=src_t: e.matmul(
                            out=psb[pi][:TB, :], lhsT=src_t[:, kc, blk * TB:(blk + 1) * TB], rhs=wv[:, kc, :],
                            start=(kc == 0), stop=(kc == 7)), [(src_n, kc), wk], [pk])
                if name == "q" and prefetch is not None:
                    pcols = prefetch.rearrange("(kc p) t -> p kc t", p=128)
                    dma("sp", xstage[:].rearrange("p a c -> p (a c)").rearrange("p (k t) -> p k t", k=4), pcols[:, 0:4, :],
                        [], xkeys(0) + xkeys(1), ("stgA", 0))
                    dma("sp", xg[:].rearrange("p k t -> p (k t)").bitcast(F32).rearrange("p (k t) -> p k t", k=4), pcols[:, 4:8, :],
                        [], [("xg", j) for j in range(8)], ("stgB", 0))
                if dq_:
                    rbi, rbk = bank()
                    for blk in range(NB):
                        S.op("pe", lambda e, blk=blk, rbi=rbi: e.matmul(
                            out=psb[rbi][:TB, blk:blk + 1], lhsT=rs1[0:1, blk * TB:(blk + 1) * TB], rhs=ident_f[0:1, 0:1],
                            start=True, stop=True), ["rs1", "ident_f"], [rbk])
                    S.op("act", lambda e, rbi=rbi: e.activation(out=rtok[:TB, 0:NB], in_=psb[rbi][:TB, 0:NB], func=AF.Copy),
                         [rbk], ["rtok"])
                for blk in range(NB):
                    pi, pk = pbanks[blk]
                    ps = psb[pi]
                    if name in ("q", "k"):
                        At = Aq if name == "q" else Ak
                        ak = ("A" + name, blk)
                        bti, btk = rot_bt.next()
                        psv = ps[:TB, :].rearrange("p (h two d) -> p h two d", h=H, two=2)
                        Bv = Btmp[:TB, bti, :].rearrange("p (h two d) -> p h two d", h=H, two=2)
                        if dq_:
                            rsc = rtok[:TB, blk:blk + 1]
                            S.op("dve", lambda e, ps=ps, At=At, blk=blk, rsc=rsc: e.scalar_tensor_tensor(
                                out=At[:TB, blk, :].rearrange("p (h d) -> p h d", h=H),
                                in0=ps[:TB, :].rearrange("p (h d) -> p h d", h=H), scalar=rsc,
                                in1=C2[:TB, blk:blk + 1, :].broadcast_to([TB, H, 128]), op0=ALU.mult, op1=ALU.mult),
                                [pk, "rope_c", "rtok"], [ak])
                            S.op("dve", lambda e, psv=psv, Bv=Bv, blk=blk, rsc=rsc: e.scalar_tensor_tensor(
                                out=Bv[:, :, 0, :], in0=psv[:, :, 1, :], scalar=rsc,
                                in1=S2[:TB, blk:blk + 1, 0:64].broadcast_to([TB, H, 64]), op0=ALU.mult, op1=ALU.mult),
                                [pk, "rope_s", "rtok"], [(btk, 0)])
                            S.op("dve", lambda e, psv=psv, Bv=Bv, blk=blk, rsc=rsc: e.scalar_tensor_tensor(
                                out=Bv[:, :, 1, :], in0=psv[:, :, 0, :], scalar=rsc,
                                in1=S2[:TB, blk:blk + 1, 64:128].broadcast_to([TB, H, 64]), op0=ALU.mult, op1=ALU.mult),
                                [pk, "rope_s", "rtok"], [(btk, 1)])
                        else:
                            S.op("dve", lambda e, ps=ps, At=At, blk=blk: e.tensor_tensor(
                                out=At[:TB, blk, :].rearrange("p (h d) -> p h d", h=H),
                                in0=ps[:TB, :].rearrange("p (h d) -> p h d", h=H),
                                in1=C2[:TB, blk:blk + 1, :].broadcast_to([TB, H, 128]), op=ALU.mult),
                                [pk, "rope_c"], [ak])
                            S.op("dve", lambda e, psv=psv, Bv=Bv, blk=blk: e.tensor_tensor(
                                out=Bv[:, :, 0, :], in0=psv[:, :, 1, :],
                                in1=S2[:TB, blk:blk + 1, 0:64].broadcast_to([TB, H, 64]), op=ALU.mult),
                                [pk, "rope_s"], [(btk, 0)])
                            S.op("dve", lambda e, psv=psv, Bv=Bv, blk=blk: e.tensor_tensor(
                                out=Bv[:, :, 1, :], in0=psv[:, :, 0, :],
                                in1=S2[:TB, blk:blk + 1, 64:128].broadcast_to([TB, H, 64]), op=ALU.mult),
                                [pk, "rope_s"], [(btk, 1)])
                        S.op(PEng[0], lambda e, At=At, blk=blk, bti=bti: e.tensor_tensor(
                            out=At[:TB, blk, :], in0=At[:TB, blk, :], in1=Btmp[:TB, bti, :], op=ALU.add),
                            [ak, (btk, 0), (btk, 1)], [ak])
                    else:
                        S.op("act", lambda e, ps=ps, blk=blk: e.activation(out=v_sb[:TB, blk, :], in_=ps[:TB, :], func=AF.Copy),
                             [pk], [("v_sb", blk)])
                        for h in range(H):
                            S.op("act", lambda e, ps=ps, blk=blk, h=h: e.activation(
                                out=vdec[:TB, blk, h * 128:(h + 1) * 128], in_=ps[:TB, h * 128:(h + 1) * 128],
                                func=AF.Identity, scale=dkcol[:TB, h:h + 1]), [pk, "dkcol"], [("vdec", blk)])

            wv, wk = w_next("in", l, 8)
            for hc in range(H):
                pi, pk = bank()
                ps = psb[pi]
                for kc in range(8):
                    S.op("pe", lambda e, kc=kc, hc=hc, ps=ps, wv=wv: e.matmul(
                        out=ps[:, :T], lhsT=wv[:, kc, hc * 128:(hc + 1) * 128], rhs=xn[:, kc, :T],
                        start=(kc == 0), stop=(kc == 7)), [("xn", kc), wk], [pk])
                S.op("act", lambda e, ps=ps, hc=hc: e.activation(out=qxT[:, hc, :T], in_=ps[:, :T], func=AF.Copy),
                     [pk], [("qxT", hc)])

            def conv_cc(cc):
                wv, wk = w_next("in", l, cc)
                pcs = []
                for s_i in range(4):
                    pi, pk = bank()
                    pcs.append((pi, pk))
                    for kc in range(8):
                        S.op("pe", lambda e, pi=pi, kc=kc, s_i=s_i, wv=wv: e.matmul(
                            out=psb[pi][:, :T], lhsT=wv[:, kc, s_i * 128:(s_i + 1) * 128], rhs=xn[:, kc, :T],
                            start=(kc == 0), stop=(kc == 7)), [("xn", kc), wk], [pk])
                (pb, pbk), (pc, pck), (ph, phk), (pz, pzk) = pcs
                ui, uk = rot_u.next()
                s0i, s0k = rot_s0.next()
                S.op("act", lambda e, pc=pc: e.activation(out=c_sb[:, :T], in_=psb[pc][:, :T], func=AF.Copy), [pck], ["c_sb"])
                S.op("dve", lambda e, ph=ph, ui=ui: e.tensor_tensor(out=ubuf[:, ui, 2:2 + T], in0=psb[ph][:, :T],
                                                                    in1=c_sb[:, :T], op=ALU.mult), [phk, "c_sb"], [uk])
                tk = ("utail", l, cc)
                S.op(PEng[0], lambda e, ui=ui, cc=cc: e.tensor_copy(out=ubuf[:, ui, 0:2], in_=utail[:, l, cc, :]), [tk], [(uk, "t")])
                S.op(PEng[0], lambda e, ui=ui, cc=cc: e.tensor_copy(out=utail[:, l, cc, :], in_=ubuf[:, ui, T:T + 2]), [uk], [tk])
                S.op("act", lambda e, ui=ui, s0i=s0i, cc=cc: e.activation(out=s0b[:, s0i, :T], in_=ubuf[:, ui, 0:T], func=AF.Identity,
                                                                          scale=convw[:, l, 0, cc:cc + 1]), [uk, (uk, "t"), "convw"], [s0k])
                S.op("act", lambda e, ui=ui, cc=cc: e.activation(out=s1b[:, :T], in_=ubuf[:, ui, 1:T + 1], func=AF.Identity,
                                                                 scale=convw[:, l, 1, cc:cc + 1]), [uk, (uk, "t"), "convw"], ["s1b"])
                S.op("act", lambda e, ui=ui, cc=cc: e.activation(out=s2b[:, :T], in_=ubuf[:, ui, 2:T + 2], func=AF.Identity,
                                                                 scale=convw[:, l, 2, cc:cc + 1]), [uk, "convw"], ["s2b"])
                S.op(PEng[0], lambda e, s0i=s0i: e.tensor_tensor(out=s0b[:, s0i, :T], in0=s0b[:, s0i, :T], in1=s1b[:, :T], op=ALU.add),
                     [s0k, "s1b"], [s0k])
                S.op(PEng[0], lambda e, s0i=s0i: e.tensor_tensor(out=s0b[:, s0i, :T], in0=s0b[:, s0i, :T], in1=s2b[:, :T], op=ALU.add),
                     [s0k, "s2b"], [s0k])
                szi, szk = sigmoid_gate(pz, pzk, T)
                S.op("dve", lambda e, pb=pb, szi=szi: e.tensor_tensor(out=szb[:, szi, :T], in0=psb[pb][:, :T], in1=szb[:, szi, :T],
                                                                      op=ALU.mult), [pbk, szk], [szk])
                S.op(PEng[0], lambda e, s0i=s0i, szi=szi, cc=cc: e.tensor_tensor(out=yT[:, cc, :T], in0=s0b[:, s0i, :T],
                                                                                in1=szb[:, szi, :T], op=ALU.mult),
                     [s0k, szk], [("yT", cc)])

            for name, At, dst, dtab, dkey in (("q", Aq, qT, dqb, "dqb"), ("k", Ak, kT, dkb, "dkb")):
                for h in range(H):
                    pi, pk = bank()
                    ps = psb[pi]
                    for blk in range(NB):
                        S.op("pe", lambda e, ps=ps, At=At, blk=blk, h=h: e.matmul(
                            out=ps[:, blk * TB:(blk + 1) * TB], lhsT=At[:TB, blk, h * 128:(h + 1) * 128],
                            rhs=ident_b[:TB, :TB], start=True, stop=True), [("A" + name, blk), "ident_b"], [pk])
                    S.op("dve", lambda e, ps=ps, dst=dst, dtab=dtab, h=h: e.tensor_tensor(
                        out=dst[:, h, :T].rearrange("p (n t) -> p n t", t=TB),
                        in0=ps[:, :T].rearrange("p (n t) -> p n t", t=TB),
                        in1=dtab[:, h:h + 1, :TB].broadcast_to([128, NB, TB]), op=ALU.mult),
                        [pk, dkey], [(name + "T", h)])

            def mem_scores(h):
                pti, ptk = rot_pt.next()
                for mc in range(2):
                    pi, pk = bank()
                    ps = psb[pi]
                    S.op("pe", lambda e, ps=ps, h=h, mc=mc: e.matmul(
                        out=ps[:, :T], lhsT=KT[:, l, h, mc * 128:(mc + 1) * 128], rhs=qxT[:, h, :T],
                        start=True, stop=True), [("KT", l), ("qxT", h)], [pk])
                    S.op("act", lambda e, ps=ps, pti=pti, mc=mc: e.activation(
                        out=PT[:, pti, mc, :T], in_=ps[:, :T], func=AF.Exp, scale=float(HD) ** -0.5), [pk], [(ptk, mc)])
                return pti, ptk

            def mem_pv(h, pti, ptk):
                oi, ok = bank()
                di, dk_ = bank()
                for mc in range(2):
                    S.op("pe", lambda e, di=di, pti=pti, mc=mc: e.matmul(
                        out=psb[di][:, :T], lhsT=ones_1[:], rhs=PT[:, pti, mc, :T],
                        start=(mc == 0), stop=(mc == 1)), ["ones_1", (ptk, mc)], [dk_])
                for mc in range(2):
                    S.op("pe", lambda e, oi=oi, pti=pti, mc=mc, h=h: e.matmul(
                        out=psb[oi][:, :T], lhsT=Vm[:, l, mc, h * 128:(h + 1) * 128], rhs=PT[:, pti, mc, :T],
                        start=(mc == 0), stop=(mc == 1)), [("Vm", l), (ptk, mc)], [ok])
                oxn = scrA[:, h * 512:h * 512 + T]
                ri, rk = rot_rs.next()
                S.op("act", lambda e, di=di: e.activation(out=psb[di][:, :T], in_=psb[di][:, :T], func=AF.Ln), [dk_], [dk_])
                S.op("act", lambda e, di=di, ri=ri: e.activation(out=rsb[:, ri, :T], in_=psb[di][:, :T], func=AF.Exp, scale=-1.0),
                     [dk_], [rk])
                S.op("dve", lambda e, oi=oi, ri=ri, oxn=oxn: e.tensor_tensor(out=oxn, in0=psb[oi][:, :T], in1=rsb[:, ri, :T],
                                                                              op=ALU.mult), [ok, rk], [("scrA", h)])

            mp = [None] * H
            mp[0] = mem_scores(0)
            mp[1] = mem_scores(1)
            mem_pv(0, *mp[0])
            mp[2] = mem_scores(2)
            mem_pv(1, *mp[1])
            mp[3] = mem_scores(3)
            mem_pv(2, *mp[2])
            mem_pv(3, *mp[3])

            wz = {}

            def z_gate(which, h):
                wi, base = (9, 8) if which == "x" else (7, 4)
                if which not in wz:
                    wz[which] = w_next("in", l, wi)
                wv, wk = wz[which]
                pi, pk = bank()
                for kc in range(8):
                    S.op("pe", lambda e, pi=pi, kc=kc, h=h, wv=wv: e.matmul(
                        out=psb[pi][:, :T], lhsT=wv[:, kc, h * 128:(h + 1) * 128], rhs=xn[:, kc, :T],
                        start=(kc == 0), stop=(kc == 7)), [("xn", kc), wk], [pk])
                szi, szk = sigmoid_gate(pi, pk, T)
                if which == "x":
                    src_ap, src_key = scrA[:, h * 512:h * 512 + T], ("scrA", h)
                else:
                    src_ap, src_key = orn[:, h, :T], ("orn", h)
                S.op(PEng[0], lambda e, szi=szi, h=h, base=base, src_ap=src_ap: e.tensor_tensor(
                    out=yT[:, base + h, :T], in0=src_ap, in1=szb[:, szi, :T], op=ALU.mult),
                    [szk, src_key], [("yT", base + h)])

            obanks = [bank(hold=True) for _ in range(H)]
            skey = ("S", l)
            zx_left = list(range(H))
            per = (H + NB - 1) // NB
            def ret_scores(blk):
                sci, sck = bank()
                for h in range(H):
                    S.op("pe", lambda e, sci=sci, h=h, blk=blk: e.matmul(
                        out=psb[sci][:TB, h * TB:(h + 1) * TB], lhsT=kT[:, h, blk * TB:(blk + 1) * TB],
                        rhs=qT[:, h, blk * TB:(blk + 1) * TB], start=True, stop=True),
                        [("kT", h), ("qT", h)], [sck])
                mi, mk_ = rot_scm.next()
                S.op("dve", lambda e, sci=sci, mi=mi: e.tensor_tensor(
                    out=scm[:TB, mi, :, :TB], in0=psb[sci][:TB, :H * TB].rearrange("p (h t) -> p h t", h=H),
                    in1=mask_b[:TB, 0:TB].unsqueeze(1).broadcast_to([TB, H, TB]), op=ALU.mult),
                    [sck, "mask_b"], [mk_])
                ki, kk = bank()
                for h in range(H):
                    S.op("pe", lambda e, ki=ki, h=h, blk=blk: e.matmul(
                        out=psb[ki][:, h * 128:(h + 1) * 128], lhsT=Ak[:TB, blk, h * 128:(h + 1) * 128],
                        rhs=vdec[:TB, blk, h * 128:(h + 1) * 128], start=True, stop=True),
                        [("Ak", blk), ("vdec", blk)], [kk])
                return mi, mk_, ki, kk

            def ret_out(blk, mi, mk_, ki, kk, sbi, sbk):
                for h in range(H):
                    oi, ok = obanks[h]
                    S.op("pe", lambda e, oi=oi, mi=mi, h=h, blk=blk: e.matmul(
                        out=psb[oi][:, blk * TB:(blk + 1) * TB], lhsT=v_sb[:TB, blk, h * 128:(h + 1) * 128],
                        rhs=scm[:TB, mi, h, :TB], start=True, stop=False), [("v_sb", blk), mk_], [ok])
                    S.op("pe", lambda e, oi=oi, sbi=sbi, h=h, blk=blk: e.matmul(
                        out=psb[oi][:, blk * TB:(blk + 1) * TB], lhsT=Sbf[:, sbi, h, :],
                        rhs=qT[:, h, blk * TB:(blk + 1) * TB], start=False, stop=True), [sbk, ("qT", h)], [ok])

            def ret_state(blk, ki, kk):
                Sl = Sst[:, l].rearrange("p h e -> p (h e)")
                S.op("dve", lambda e, ki=ki, Sl=Sl: e.tensor_tensor(out=Sl, in0=psb[ki][:, :], in1=Sl, op=ALU.add),
                     [kk, skey], [skey])
                nsbi, nsbk = None, None
                if blk < NB - 1:
                    nsbi, nsbk = rot_sbf.next()
                    S.op("dve", lambda e, Sl=Sl, gtab=gtab, nsbi=nsbi: e.tensor_tensor(
                        out=Sbf[:, nsbi].rearrange("p h e -> p (h e)"), in0=Sl,
                        in1=gtab[:].rearrange("p h e -> p (h e)"), op=ALU.mult), [skey, gtab_key], [nsbk])
                S.op(PEng[0], lambda e, Sl=Sl, gtab=gtab: e.tensor_tensor(
                    out=Sl, in0=Sl, in1=gtab[:].rearrange("p h e -> p (h e)"), op=ALU.mult), [skey, gtab_key], [skey])
                return nsbi, nsbk

            cur = ret_scores(0)
            for blk in range(NB):
                nxt = ret_scores(blk + 1) if blk + 1 < NB else None
                ret_out(blk, cur[0], cur[1], cur[2], cur[3], sbi, sbk)
                sbi, sbk = ret_state(blk, cur[2], cur[3])
                cur = nxt

            sq = []
            for h in range(H):
                oi, ok = obanks[h]
                qi, qk = rot_osq.next()
                S.op("act", lambda e, oi=oi, qi=qi: e.activation(out=osq[:, qi, :T], in_=psb[oi][:, :T], func=AF.Square),
                     [ok], [qk])
                sq.append((qi, qk))
            for h in range(H):
                oi, ok = obanks[h]
                qi, qk = sq[h]
                mi2, mk2 = bank()
                S.op("pe", lambda e, mi2=mi2, qi=qi: e.matmul(out=psb[mi2][:, :T], lhsT=ones_h[:], rhs=osq[:, qi, :T],
                                                             start=True, stop=True), ["ones_h", qk], [mk2])
                ri, rk = rot_rs.next()
                S.op("act", lambda e, mi2=mi2: e.activation(out=psb[mi2][:, :T], in_=psb[mi2][:, :T], func=AF.Ln, bias=EPS),
                     [mk2], [mk2])
                S.op("act", lambda e, mi2=mi2, ri=ri: e.activation(out=rsb[:, ri, :T], in_=psb[mi2][:, :T], func=AF.Exp, scale=-0.5),
                     [mk2], [rk])
                S.op("dve", lambda e, oi=oi, ri=ri, h=h: e.tensor_tensor(out=orn[:, h, :T], in0=psb[oi][:, :T],
                                                                         in1=rsb[:, ri, :T], op=ALU.mult),
                     [ok, rk], [("orn", h)])
                release(oi)
            for cc in range(4):
                conv_cc(cc)
            for h in range(H):
                z_gate("x", h)
            for h in range(H):
                z_gate("r", h)

            pre_stats = None
            if prefetch is not None:
                pre_stats = rms_stats(lambda kc: stage_view(kc)[0], lambda kc: stage_view(kc)[1][0], 8, T, ones_rms, "ones_rms", EPS)

            nbi, nbk = bank(hold=True)
            pend = []
            nxt_defer = (l < DEPTH - 1) and DEFERQ

            def stats_part(oc):
                si, sk = rot_xsq.next()
                S.op("act", lambda e, oc=oc, si=si: e.activation(out=xsq[:, si, :T], in_=xT[:, oc, :T], func=AF.Square),
                     [("xT", oc)], [sk])
                S.op("pe", lambda e, oc=oc, si=si: e.matmul(out=psb[nbi][:, :T], lhsT=ones_rms[:], rhs=xsq[:, si, :T],
                                                           start=(oc == 0), stop=(oc == 7)), [sk, "ones_rms"], [nbk])

            for j in range(4):
                wv, wk = w_next("out", l, j)
                for half in range(2):
                    oc = j * 2 + half
                    pi, pk = bank()
                    for kc in range(12):
                        S.op("pe", lambda e, pi=pi, kc=kc, half=half, wv=wv: e.matmul(
                            out=psb[pi][:, :T], lhsT=wv[:, kc, half * 128:(half + 1) * 128], rhs=yT[:, kc, :T],
                            start=(kc == 0), stop=(kc == 11)), [("yT", kc), wk], [pk])
                    S.op("dve", lambda e, pi=pi, oc=oc: e.tensor_tensor(out=xT[:, oc, :T], in0=psb[pi][:, :T],
                                                                        in1=xT[:, oc, :T], op=ALU.add),
                         [pk, ("xT", oc)], [("xT", oc)])
                    if pend:
                        stats_part(pend.pop(0))
                    pend.append(oc)
                    if nxt_defer:
                        S.op("act", lambda e, oc=oc: e.activation(out=xg[:, oc, :T], in_=xT[:, oc, :T], func=AF.Identity,
                                                                  scale=g_all[:, l + 1, oc:oc + 1]), [("xT", oc), g_key], [("xg", oc)])
            while pend:
                stats_part(pend.pop(0))
            S.op("act", lambda e: e.activation(out=psb[nbi][:, :T], in_=psb[nbi][:, :T], func=AF.Ln, bias=EPS), [nbk], [nbk])
            if nxt_defer:
                S.op("act", lambda e: e.activation(out=rs1[0:1, :T], in_=psb[nbi][0:1, :T], func=AF.Exp, scale=-0.5), [nbk], ["rs1"])
            S.op("act", lambda e: e.activation(out=psb[nbi][:, :T], in_=psb[nbi][:, :T], func=AF.Exp, scale=-0.5), [nbk], [nbk])
            if prefetch is not None:
                return (nbi, nbk, nxt_defer), pre_stats
            return nbi, nbk, nxt_defer

        def load_x_tile(src_cols, T, TB):
            for half in range(2):
                dma("sp", xT[:, half * 4:(half + 1) * 4, :T],
                    src_cols.rearrange("(kc p) t -> p kc t", p=128)[:, half * 4:(half + 1) * 4, :],
                    [], [("xT", half * 4 + j) for j in range(4)], ("xTld", half))

        def final_store(dst_cols, T, TB, outkey, stats=None):
            if stats is None:
                bi, bk = rms_stats(lambda kc: xT[:, kc, :T], lambda kc: ("xT", kc), 8, T, ones_rms, "ones_rms", EPS)
            else:
                bi, bk = stats[0], stats[1]
            ystA = scrA[:].rearrange("p (k t) -> p k t", k=4)
            for kc in range(8):
                dst = ystA[:, kc, :T] if kc < 4 else orn[:, kc - 4, :T]
                key = ("scrA", kc) if kc < 4 else ("orn", kc - 4)
                S.op("dve", lambda e, kc=kc, bi=bi, dst=dst: e.scalar_tensor_tensor(
                    out=dst, in0=xT[:, kc, :T], scalar=gfin[:, kc:kc + 1], in1=psb[bi][:, :T],
                    op0=ALU.mult, op1=ALU.mult), [("xT", kc), bk, "gfin"], [key])
            release(bi)
            dv = dst_cols.rearrange("(kc p) t -> p kc t", p=128)
            toks = [dma("act", dv[:, 0:4, :], ystA[:, :, :T], [("scrA", h) for h in range(4)], [(outkey, 0)], ("yst", 0)),
                    dma("act", dv[:, 4:8, :], orn[:, :, :T], [("orn", h) for h in range(4)], [(outkey, 1)], ("yst", 1))]
            return toks

        out_toks = []

        STAGE = int(os.environ.get("MK_STAGE", "9"))
        def prep_kT(l):
            for hp in range(2):
                pi, pk = bank()
                for hh in range(2):
                    h = hp * 2 + hh
                    for mc in range(2):
                        S.op("pe", lambda e, pi=pi, hh=hh, h=h, mc=mc: e.matmul(
                            out=psb[pi][:, hh * 256 + mc * 128: hh * 256 + (mc + 1) * 128],
                            lhsT=Kb[:, mc, h * 128:(h + 1) * 128], rhs=ident_b[:], start=True, stop=True),
                            [("Kb", mc), "ident_b"], [pk])
                S.op("act", lambda e, pi=pi, hp=hp: e.activation(
                    out=KT[:, l, hp * 2:hp * 2 + 2, :], in_=psb[pi][:, :].rearrange("p (h m) -> p h m", h=2), func=AF.Copy),
                    [pk], [("KT", l)])

        if n_ptiles > 0 and STAGE >= 4:
            for l in range(DEPTH):
                S.op("dve", lambda e, l=l: e.memset(Sst[:, l].rearrange("p h e -> p (h e)"), 0.0), [], [("S", l)])
            S.op("dve", lambda e: e.memset(utail[:].rearrange("p l c t -> p (l c t)"), 0.0), [],
                 [("utail", l, cc) for l in range(DEPTH) for cc in range(4)])
            SUB = int(os.environ.get("MK_SUB", "9"))
            memT = scrA[:].rearrange("p (k m) -> p k m", k=8)
            scr_keys = [("scrA", h) for h in range(H)]
            if SUB >= 2:
                dma("sp", memT, mem_d.rearrange("(kc p) m -> p kc m", p=128), [], scr_keys, ("memld", 0))
            if SUB >= 3:
                bi, bk = rms_stats(lambda kc: memT[:, kc, :], lambda kc: ("scrA", 0), 8, NMEM, ones_rms, "ones_rms", EPS)
            for l in range(DEPTH if SUB >= 3 else 0):
                for kc in range(8):
                    S.op("dve", lambda e, kc=kc, bi=bi, l=l: e.scalar_tensor_tensor(
                        out=memn[:, kc, :], in0=memT[:, kc, :], scalar=gmem[:, l, kc:kc + 1], in1=psb[bi][:, :NMEM],
                        op0=ALU.mult, op1=ALU.mult), scr_keys + [bk, "gmem"], [("memn", kc)])
                for j, (odst, name) in enumerate(((nmk_d, "nmk"), (nmv_d, "nmv")) if SUB >= 4 else ()):
                    wv, wk = w_next("kv", l, j)
                    for mc in range(2):
                        pi, pk = bank()
                        for kc in range(8):
                            S.op("pe", lambda e, pi=pi, kc=kc, mc=mc, wv=wv: e.matmul(
                                out=psb[pi][:, :], lhsT=memn[:, kc, mc * 128:(mc + 1) * 128], rhs=wv[:, kc, :],
                                start=(kc == 0), stop=(kc == 7)), [("memn", kc), wk], [pk])
                        ki, kk = rot_kvs.next()
                        S.op("act", lambda e, pi=pi, ki=ki: e.activation(out=kvstage[:, ki, :], in_=psb[pi][:, :], func=AF.Copy),
                             [pk], [kk])
                        if os.environ.get("MK_NOKVST") != "1":
                            out_toks.append(dma(os.environ.get("MK_KVQ", "act"), odst[l][mc * 128:(mc + 1) * 128, :], kvstage[:, ki, :], [kk],
                                                [(name, l, mc)], kk))
                        if j == 0:
                            S.op("dve", lambda e, ki=ki, mc=mc: e.tensor_copy(out=Kb[:, mc, :], in_=kvstage[:, ki, :]), [kk], [("Kb", mc)])
                        else:
                            S.op("dve", lambda e, ki=ki, mc=mc, l=l: e.tensor_copy(out=Vm[:, l, mc, :], in_=kvstage[:, ki, :]),
                                 [kk], [("Vm", l)])
                    if j == 0:
                        prep_kT(l)
            if SUB >= 3:
                release(bi)

            nxt_pre = None
            pending_store = None
            PREFETCH = os.environ.get("MK_PREFETCH", "1") == "1"
            for t in range(n_ptiles if STAGE >= 5 else 0):
                dma("sp", C2[:, :, :], c2p_d[:, t * 4:(t + 1) * 4, :], [], ["rope_c"], "C2")
                dma("sp", S2[:, :, :], s2p_d[:, t * 4:(t + 1) * 4, :], [], ["rope_s"], "S2")
                if nxt_pre is None:
                    load_x_tile(xp_d[:, t * TP:(t + 1) * TP], TP, 128)
                st_ = None
                PEng[0] = "dve" if t == 0 else "pool"
                for l in range(DEPTH):
                    if l == 0 and nxt_pre is not None:
                        st_ = tile_layer(l, TP, 128, gtab_p, "gtab_p", (nxt_pre[0], nxt_pre[1], False), staged=True,
                                         after_norm=pending_store)
                        pending_store = None
                        nxt_pre = None
                    elif l == DEPTH - 1 and t + 1 < n_ptiles and PREFETCH:
                        st_, nxt_pre = tile_layer(l, TP, 128, gtab_p, "gtab_p", st_, prefetch=xp_d[:, (t + 1) * TP:(t + 2) * TP])
                    else:
                        st_ = tile_layer(l, TP, 128, gtab_p, "gtab_p", st_)
                if nxt_pre is not None:
                    def pending_store(t=t, st_=st_):
                        out_toks.extend(final_store(yp_d[:, t * TP:(t + 1) * TP], TP, 128, ("yp", t), st_))
                else:
                    out_toks += final_store(yp_d[:, t * TP:(t + 1) * TP], TP, 128, ("yp", t), st_)
            out_toks.append(dma("act", nrp_d.rearrange("l h d e -> d (l h) e"), Sst[:].rearrange("p l h e -> p (l h) e"),
                                [("S", l) for l in range(DEPTH)], ["nrp"], "Sst"))
            out_toks.append(dma("act", ncp_d, utail[:], [("utail", l, cc) for l in range(DEPTH) for cc in range(4)],
                                ["ncp"], "utail"))

        dma("sp", Sst[:].rearrange("p l h e -> p (l h) e"), sret_d.rearrange("l h d e -> d (l h) e"),
            [], [("S", l) for l in range(DEPTH)], "Sst")
        dma("sp", utail[:], sconv_d, [], [("utail", l, cc) for l in range(DEPTH) for cc in range(4)], "utail")
        dma("sp", C2[:32, 0:1, :], c2s_d, [], ["rope_c"], "C2")
        dma("sp", S2[:32, 0:1, :], s2s_d, [], ["rope_s"], "S2")

        for l in range(DEPTH if STAGE >= 1 else 0):
            dma("sp", xstage[:, 0, :].rearrange("p (mc c) -> p mc c", mc=2), cmk_d[l].rearrange("(mc p) c -> p mc c", p=128),
                [], xkeys(0), ("xstage", 0))
            S.op("dve", lambda e: e.tensor_copy(out=Kb[:].rearrange("p mc c -> p (mc c)"), in_=xstage[:, 0, :]),
                 xkeys(0), [("Kb", 0), ("Kb", 1)])
            prep_kT(l)
            dma("sp", xstage[:, 1, :].rearrange("p (mc c) -> p mc c", mc=2), cmv_d[l].rearrange("(mc p) c -> p mc c", p=128),
                [], xkeys(1), ("xstage", 1))
            S.op("dve", lambda e, l=l: e.tensor_copy(out=Vm[:, l].rearrange("p mc c -> p (mc c)"), in_=xstage[:, 1, :]),
                 xkeys(1), [("Vm", l)])

        if STAGE >= 2:
            load_x_tile(xs_d[:, :], 32, 32)
        NLS = int(os.environ.get("MK_NLS", DEPTH))
        st_ = None
        PEng[0] = "pool"
        if STAGE >= 3:
            for l in range(NLS):
                st_ = tile_layer(l, 32, 32, gtab_s, "gtab_s", st_)
        if STAGE >= 2:
            out_toks += final_store(ys_d[:, :], 32, 32, "ys", st_)
        out_toks.append(dma("act", nrs_d.rearrange("l h d e -> d (l h) e"), Sst[:].rearrange("p l h e -> p (l h) e"),
                            [("S", l) for l in range(DEPTH)], ["nrs"], "Sst"))
        out_toks.append(dma("act", ncs_d, utail[:], [("utail", l, cc) for l in range(DEPTH) for cc in range(4)],
                            ["ncs"], "utail"))

        S.final_wait("act", out_toks)
        S.final_wait("sp", [S.last_writer[("wring", s)] for s in range(NW) if ("wring", s) in S.last_writer])
        S.emit(st)
    return nc


def _tables():
    f32 = np.float32
    freqs = (np.float32(10000.0) ** (-np.arange(0, HD, 2, dtype=np.float32) / np.float32(HD))).astype(f32)

    def cs(pos):
        ang = (pos.astype(f32)[:, None] * freqs[None, :]).astype(f32).astype(np.float64)
        c, s = np.cos(ang), np.sin(ang)
        c2 = np.concatenate([c, c], axis=1).astype(f32)
        s2 = np.concatenate([-s, s], axis=1).astype(f32)
        return c2, s2

    c2p, s2p = cs(np.arange(SEQ))
    c2p = np.ascontiguousarray(c2p.reshape(SEQ // 128, 128, 128).transpose(1, 0, 2))
    s2p = np.ascontiguousarray(s2p.reshape(SEQ // 128, 128, 128).transpose(1, 0, 2))
    c2s, s2s = cs(PAST + np.arange(32))
    c2s = np.ascontiguousarray(c2s.reshape(32, 1, 128))
    s2s = np.ascontiguousarray(s2s.reshape(32, 1, 128))
    gam = 1.0 - np.power(2.0, -5.0 - np.arange(H, dtype=np.float64))
    lg = np.log(gam)
    i = np.arange(128, dtype=np.float64)
    dq = np.exp((i[None, :] + 1) * lg[:, None])
    dk = np.exp(-(i[None, :] + 1) * lg[:, None]) * (HD ** -0.5)
    dqb = np.ascontiguousarray(np.broadcast_to(dq[None], (128, H, 128))).astype(f32)
    dkb = np.ascontiguousarray(np.broadcast_to(dk[None], (128, H, 128))).astype(f32)
    dkcol = np.ascontiguousarray(dk.T).astype(f32)
    gtp = np.ascontiguousarray(np.broadcast_to(np.exp(128 * lg)[None, :, None], (128, H, 128))).astype(f32)
    gts = np.ascontiguousarray(np.broadcast_to(np.exp(32 * lg)[None, :, None], (128, H, 128))).astype(f32)
    mask = (np.arange(128)[None, :] >= np.arange(128)[:, None]).astype(f32)
    return dict(c2p=c2p, s2p=s2p, c2s=c2s, s2s=s2s, dqb=dqb, dkb=dkb, dkcol=dkcol, gtab_p=gtp, gtab_s=gts,
                mask=mask, ident=np.eye(128, dtype=f32))


_NC_CACHE = {}


def kernel(x_prompt, x_sample, mem_prompt, state_conv, state_ret, cache_mem_k, cache_mem_v,
           norm_g, w_in, conv_w, mem_norm_g, w_mem_kv, w_out, final_norm_g):
    n_ptiles = int(os.environ.get("MK_NPT", SEQ // TP))
    f32 = np.float32
    A = lambda a: np.ascontiguousarray(np.asarray(a, dtype=f32))
    x_prompt, x_sample, mem_prompt = A(x_prompt), A(x_sample), A(mem_prompt)
    state_conv, state_ret = A(state_conv), A(state_ret)
    cache_mem_k, cache_mem_v = A(cache_mem_k), A(cache_mem_v)
    w_in, w_mem_kv, w_out = A(w_in), A(w_mem_kv), A(w_out)
    tabs = _tables()
    g_all = np.ascontiguousarray(A(norm_g).reshape(DEPTH, 8, 128).transpose(2, 0, 1))
    gmem = np.ascontiguousarray(A(mem_norm_g).reshape(DEPTH, 8, 128).transpose(2, 0, 1))
    gfin = np.ascontiguousarray(A(final_norm_g).reshape(8, 128).transpose(1, 0))
    convw = np.ascontiguousarray(A(conv_w).reshape(DEPTH, 3, 4, 128).transpose(3, 0, 1, 2))
    zeros_p = np.zeros((D, SEQ), f32)
    PC = [0, 1, 4, 5]
    in_maps = []
    for c in range(NCORES):
        m = dict(tabs)
        m["xp"] = np.ascontiguousarray(x_prompt[PC.index(c)].T) if c in PC else zeros_p
        m["xs"] = np.ascontiguousarray(x_sample[c].T)
        m["mem"] = np.ascontiguousarray(mem_prompt[PC.index(c) if c in PC else 0].T)
        m["sconv"] = np.ascontiguousarray(state_conv[:, c].reshape(DEPTH, 2, 4, 128).transpose(3, 0, 2, 1))
        m["sret"] = np.ascontiguousarray(state_ret[:, c])
        m["cmk"] = np.ascontiguousarray(cache_mem_k[:, c].reshape(DEPTH, NMEM, 512))
        m["cmv"] = np.ascontiguousarray(cache_mem_v[:, c].reshape(DEPTH, NMEM, 512))
        m["w_in"], m["w_kv"], m["w_out"] = w_in, w_mem_kv, w_out
        m["g_all"], m["gmem"], m["gfin"], m["convw"] = g_all, gmem, gfin, convw
        in_maps.append(m)
    if n_ptiles not in _NC_CACHE:
        _NC_CACHE[n_ptiles] = build_program(n_ptiles)
    nc = _NC_CACHE[n_ptiles]
    res = run_bass_kernel_spmd(nc, in_maps, core_ids=list(range(NCORES)))
    R = res.results
    unconv = lambda a: np.ascontiguousarray(np.asarray(a).transpose(1, 3, 2, 0).reshape(DEPTH, 2, 512))
    y_prompt = np.stack([np.asarray(R[c]["yp"]).T for c in PC]).astype(f32)
    y_sample = np.stack([np.asarray(R[c]["ys"]).T for c in range(8)]).astype(f32)
    ncp = np.stack([unconv(R[c]["ncp"]) for c in PC], axis=1).astype(f32)
    nrp = np.stack([np.asarray(R[c]["nrp"]) for c in PC], axis=1).astype(f32)
    nmk = np.stack([np.asarray(R[c]["nmk"]).reshape(DEPTH, NMEM, H, HD) for c in PC], axis=1).astype(f32)
    nmv = np.stack([np.asarray(R[c]["nmv"]).reshape(DEPTH, NMEM, H, HD) for c in PC], axis=1).astype(f32)
    ncs = np.stack([unconv(R[c]["ncs"]) for c in range(8)], axis=1).astype(f32)
    nrs = np.stack([np.asarray(R[c]["nrs"]) for c in range(8)], axis=1).astype(f32)
    return (y_prompt, y_sample, ncp, nrp, nmk, nmv, ncs, nrs)
```

```python
import os
import numpy as np
from contextlib import ExitStack
import concourse.bass as bass
import concourse.mybir as mybir
from concourse.bass_utils import run_bass_kernel_spmd

F32 = mybir.dt.float32
BF16 = mybir.dt.bfloat16
AF = mybir.ActivationFunctionType
ALU = mybir.AluOpType

D = 1024
DEPTH = 4
SEQ = 8192
NMEM = 256
H = 4
HD = 128
EPS = 1e-6
PAST = 1024
TP = 512
NCORES = 8


class _Op:
    __slots__ = ("eng", "fn", "waits", "sem", "count", "is_dma")


class Sched:
    ENGS = ("pe", "act", "dve", "pool", "sp")

    def __init__(self, nc):
        self.nc = nc
        self.ops = {e: [] for e in self.ENGS}
        self.count = {e: 0 for e in self.ENGS}
        self.last_writer = {}
        self.readers = {}
        self.seen = {e: {} for e in self.ENGS}
        self.dma_count = {}
        self.sem_handles = {}

    def _need(self, eng, tok, waits, same_ok):
        semkey, cnt, src_eng = tok
        if src_eng == eng and same_ok:
            return
        if self.seen[eng].get(semkey, 0) >= cnt:
            return
        self.seen[eng][semkey] = cnt
        for i, (sk, c) in enumerate(waits):
            if sk == semkey:
                waits[i] = (sk, max(c, cnt))
                return
        waits.append((semkey, cnt))

    def op(self, eng, fn, reads=(), writes=(), dma=None):
        o = _Op()
        o.eng = eng
        o.fn = fn
        o.waits = []
        o.is_dma = dma is not None
        same_ok = (eng == "pe") and not o.is_dma
        for k in reads:
            w = self.last_writer.get(k)
            if w is not None:
                self._need(eng, w, o.waits, same_ok)
        for k in writes:
            w = self.last_writer.get(k)
            if w is not None:
                self._need(eng, w, o.waits, same_ok)
            for r in self.readers.get(k, ()):
                self._need(eng, r, o.waits, same_ok)
        if o.is_dma:
            semkey = ("dma", eng, dma)
            self.dma_count[semkey] = self.dma_count.get(semkey, 0) + 1
            cnt = 16 * self.dma_count[semkey]
            src = None
        else:
            semkey = eng
            self.count[eng] += 1
            cnt = self.count[eng]
            src = eng
        o.sem = semkey
        o.count = cnt
        tok = (semkey, cnt, src)
        for k in reads:
            self.readers.setdefault(k, []).append(tok)
        for k in writes:
            self.last_writer[k] = tok
            self.readers[k] = []
        self.ops[eng].append(o)
        return tok

    def final_wait(self, eng, toks):
        o = _Op()
        o.eng = eng
        o.fn = None
        o.waits = []
        o.is_dma = False
        o.sem = None
        o.count = 0
        for t in toks:
            self._need(eng, t, o.waits, False)
        self.ops[eng].append(o)

    def emit(self, stack):
        nc = self.nc
        semkeys = set()
        for e in self.ENGS:
            for o in self.ops[e]:
                if o.sem is not None:
                    semkeys.add(o.sem)
                for sk, _ in o.waits:
                    semkeys.add(sk)
        for i, sk in enumerate(sorted(semkeys, key=str)):
            self.sem_handles[sk] = stack.enter_context(nc.semaphore("s%d" % i))
        block = stack.enter_context(nc.Block())
        sems = self.sem_handles

        def make(engkey):
            ops = self.ops[engkey]

            def body(e):
                for o in ops:
                    for sk, c in o.waits:
                        e.wait_ge(sems[sk], c)
                    if o.fn is None:
                        continue
                    ins = o.fn(e)
                    ins.then_inc(sems[o.sem], 16 if o.is_dma else 1)
            return body

        block.sync(make("sp"))
        block.scalar(make("act"))
        block.vector(make("dve"))
        block.gpsimd(make("pool"))
        block.tensor(make("pe"))


class Rot:
    def __init__(self, name, n):
        self.name, self.n, self.i = name, n, 0

    def next(self):
        i = self.i % self.n
        self.i += 1
        return i, (self.name, i)


def build_program(n_ptiles):
    nc = bass.Bass("TRN2", target_bir_lowering=False)
    st = ExitStack()

    def din(name, shape, dt=F32):
        return nc.dram_tensor(name, list(shape), dt, kind="ExternalInput").ap()

    def dout(name, shape, dt=F32):
        return nc.dram_tensor(name, list(shape), dt, kind="ExternalOutput").ap()

    xp_d = din("xp", [D, SEQ])
    xs_d = din("xs", [D, 32])
    mem_d = din("mem", [D, NMEM])
    sconv_d = din("sconv", [128, DEPTH, 4, 2])
    sret_d = din("sret", [DEPTH, H, HD, HD])
    cmk_d = din("cmk", [DEPTH, NMEM, 512])
    cmv_d = din("cmv", [DEPTH, NMEM, 512])
    w_in_d = din("w_in", [DEPTH, D, 5120])
    w_kv_d = din("w_kv", [DEPTH, D, 1024])
    w_out_d = din("w_out", [DEPTH, 1536, D])
    g_d = din("g_all", [128, DEPTH, 8])
    gmem_d = din("gmem", [128, DEPTH, 8])
    gfin_d = din("gfin", [128, 8])
    convw_d = din("convw", [128, DEPTH, 3, 4])
    ident_d = din("ident", [128, 128])
    mask_d = din("mask", [128, 128])
    dqb_d = din("dqb", [128, H, 128])
    dkb_d = din("dkb", [128, H, 128])
    dkcol_d = din("dkcol", [128, H])
    gp_d = din("gtab_p", [128, H, 128])
    gs_d = din("gtab_s", [128, H, 128])
    c2p_d = din("c2p", [128, SEQ // 128, 128])
    s2p_d = din("s2p", [128, SEQ // 128, 128])
    c2s_d = din("c2s", [32, 1, 128])
    s2s_d = din("s2s", [32, 1, 128])

    yp_d = dout("yp", [D, SEQ])
    ys_d = dout("ys", [D, 32])
    ncp_d = dout("ncp", [128, DEPTH, 4, 2])
    nrp_d = dout("nrp", [DEPTH, H, HD, HD])
    nmk_d = dout("nmk", [DEPTH, NMEM, 512])
    nmv_d = dout("nmv", [DEPTH, NMEM, 512])
    ncs_d = dout("ncs", [128, DEPTH, 4, 2])
    nrs_d = dout("nrs", [DEPTH, H, HD, HD])

    winb_d = nc.dram_tensor("winb", [DEPTH, 10, 128, 8, 512], BF16).ap()
    woutb_d = nc.dram_tensor("woutb", [DEPTH, 4, 128, 12, 256], BF16).ap()
    wkvb_d = nc.dram_tensor("wkvb", [DEPTH, 2, 128, 8, 512], BF16).ap()

    with st:
        def sb(name, shape, dt=F32):
            return st.enter_context(nc.sbuf_tensor("sb_" + name, list(shape), dt))

        xT = sb("xT", [128, 8, TP])
        xn = sb("xn", [128, 8, TP], BF16)
        xsq = sb("xsq", [128, 2, TP], BF16)
        Aq = sb("Aq", [128, 4, 512], BF16)
        Btmp = sb("Btmp", [128, 2, 512], BF16)
        rsb = sb("rsb", [128, 2, TP])
        xg = sb("xg", [128, 8, TP], BF16)
        rs1 = sb("rs1", [1, TP])
        rtok = sb("rtok", [128, 4])
        Ak = sb("Ak", [128, 4, 512], BF16)
        v_sb = sb("v_sb", [128, 4, 512], BF16)
        vdec = sb("vdec", [128, 4, 512], BF16)
        qT = sb("qT", [128, H, TP], BF16)
        kT = sb("kT", [128, H, TP], BF16)
        qxT = sb("qxT", [128, H, TP], BF16)
        PT = sb("PT", [128, 2, 2, TP], BF16)
        scrA = sb("scrA", [128, 2048])
        orn = sb("orn", [128, H, TP])
        osq = sb("osq", [128, 4, TP], BF16)
        scm = sb("scm", [128, 2, H, 128], BF16)
        Sbf = sb("Sbf", [128, 2, H, 128], BF16)
        Sst = sb("Sst", [128, DEPTH, H, 128])
        utail = sb("utail", [128, DEPTH, 4, 2])
        c_sb = sb("c_sb", [128, TP])
        ubuf = sb("ubuf", [128, 2, TP + 2])
        s0b = sb("s0b", [128, 2, TP])
        s1b = sb("s1b", [128, TP])
        s2b = sb("s2b", [128, TP])
        szb = sb("szb", [128, 2, TP])
        yT = sb("yT", [128, 12, TP], BF16)
        wring = sb("wring", [128, 4, 4096], BF16)
        KT = sb("KT", [128, DEPTH, H, NMEM], BF16)
        Vm = sb("Vm", [128, DEPTH, 2, 512], BF16)
        xstage = sb("xstage", [128, 2, 1024])
        C2 = sb("C2", [128, 4, 128])
        S2 = sb("S2", [128, 4, 128])
        ident_f = sb("ident_f", [128, 128])
        ident_b = sb("ident_b", [128, 128], BF16)
        ones_rms = sb("ones_rms", [128, 128], BF16)
        ones_h = sb("ones_h", [128, 128], BF16)
        ones_1 = sb("ones_1", [128, 128], BF16)
        mask_f = sb("mask_f", [128, 128])
        mask_b = sb("mask_b", [128, 128], BF16)
        dqb = sb("dqb", [128, H, 128])
        dkb = sb("dkb", [128, H, 128])
        dkcol = sb("dkcol", [128, H])
        gtab_p = sb("gtab_p", [128, H, 128])
        gtab_s = sb("gtab_s", [128, H, 128])
        g_all = sb("g_all", [128, DEPTH, 8])
        gmem = sb("gmem", [128, DEPTH, 8])
        gfin = sb("gfin", [128, 8])
        convw = sb("convw", [128, DEPTH, 3, 4])
        memn = yT[:, 0:4, :].rearrange("p a (b m) -> p (a b) m", m=NMEM)
        Kb = yT[:, 4:6, :]
        kvstage = c_sb[:, :].rearrange("p (o t) -> p o t", o=1)

        psb = [st.enter_context(nc.psum_tensor("ps%d" % i, [128, 512], F32)) for i in range(8)]

        S = Sched(nc)
        held = set()
        bank_ctr = [0]

        def bank(hold=False):
            while True:
                i = bank_ctr[0] % 8
                bank_ctr[0] += 1
                if i not in held:
                    break
            if hold:
                held.add(i)
            return i, ("ps", i)

        def release(i):
            held.discard(i)

        rot_xsq = Rot("xsq", 2)
        rot_pt = Rot("PT", 2)
        rot_osq = Rot("osq", 4)
        rot_scm = Rot("scm", 2)
        rot_sbf = Rot("Sbf", 2)
        rot_u = Rot("ubuf", 2)
        rot_s0 = Rot("s0b", 2)
        rot_sz = Rot("szb", 2)
        rot_bt = Rot("Btmp", 2)
        rot_rs = Rot("rsb", 2)
        rot_xst = Rot("xstage", 2)
        rot_kvs = Rot("kvstage", 1)

        def xkeys(i):
            return [(("xstage", i), 0), (("xstage", i), 1)]

        def dma(q, out, in_, reads, writes, slot, **kw):
            return S.op(q, lambda e: e.dma_start(out=out, in_=in_, **kw), reads, writes, dma=slot)

        wseq = []

        def layer_wseq(l):
            return [("in", l, 4), ("in", l, 5), ("in", l, 6), ("in", l, 8),
                    ("in", l, 0), ("in", l, 1), ("in", l, 2), ("in", l, 3),
                    ("in", l, 9), ("in", l, 7),
                    ("out", l, 0), ("out", l, 1), ("out", l, 2), ("out", l, 3)]

        if n_ptiles > 0:
            for l in range(DEPTH):
                wseq += [("kv", l, 0), ("kv", l, 1)]
        for _t in range(n_ptiles):
            for l in range(DEPTH):
                wseq += layer_wseq(l)
        for l in range(DEPTH):
            wseq += layer_wseq(l)
        NW = 4
        WQ2 = os.environ.get("MK_WQ2", "0") == "1"
        wstate = {"issued": 0, "next": 0}

        def load_const(t, d, key):
            dma("sp", t[:], d, [], [key], key)

        load_const(ident_f, ident_d, "ident_f")
        load_const(mask_f, mask_d, "mask_f")
        load_const(dqb, dqb_d, "dqb")
        load_const(dkb, dkb_d, "dkb")
        load_const(dkcol, dkcol_d, "dkcol")
        load_const(gtab_p, gp_d, "gtab_p")
        load_const(gtab_s, gs_d, "gtab_s")
        load_const(g_all, g_d, "g_all")
        load_const(gmem, gmem_d, "gmem")
        load_const(gfin, gfin_d, "gfin")
        load_const(convw, convw_d, "convw")
        S.op("dve", lambda e: e.tensor_copy(out=ident_b[:], in_=ident_f[:]), ["ident_f"], ["ident_b"])
        S.op("dve", lambda e: e.tensor_copy(out=mask_b[:], in_=mask_f[:]), ["mask_f"], ["mask_b"])
        S.op("dve", lambda e: e.memset(ones_rms[:], 1.0 / D), [], ["ones_rms"])
        S.op("dve", lambda e: e.memset(ones_h[:], 1.0 / HD), [], ["ones_h"])
        S.op("dve", lambda e: e.memset(ones_1[:], 1.0), [], ["ones_1"])

        pc_ctr = [0]
        pc_last = {}
        NPC = int(os.environ.get("MK_NPC", "4"))

        def pc_dma(dst, src_ap, key):
            i = pc_ctr[0] % NPC
            pc_ctr[0] += 1
            slot = ("precast", i)
            tok = dma("pool", dst, src_ap, [("pcslot", i)], [key, ("pcslot", i)], slot)
            return tok

        def precast_layer(l):
            keys = []

            def in_block(blk, c0):
                keys.append(("winb", l, blk))
                return pc_dma(winb_d[l, blk], w_in_d[l][:, c0:c0 + 512].rearrange("(kc p) c -> p kc c", p=128),
                              ("winb", l, blk))
            for blk, c0 in ((4, 2048), (5, 2560), (6, 3072), (8, 4096)):
                in_block(blk, c0)
            for cc in range(4):
                for s_i in range(4):
                    c0 = s_i * 512 + cc * 128
                    keys.append(("winb", l, cc, s_i))
                    pc_dma(winb_d[l, cc][:, :, s_i * 128:(s_i + 1) * 128],
                           w_in_d[l][:, c0:c0 + 128].rearrange("(kc p) c -> p kc c", p=128), ("winb", l, cc, s_i))
            for blk, c0 in ((9, 4608), (7, 3584)):
                in_block(blk, c0)
            tok = None
            for j in range(4):
                keys.append(("woutb", l, j))
                tok = pc_dma(woutb_d[l, j], w_out_d[l][:, j * 256:(j + 1) * 256].rearrange("(kc p) c -> p kc c", p=128),
                             ("woutb", l, j))

        def precast_kv(l):
            tok = None
            for j in range(2):
                tok = pc_dma(wkvb_d[l, j], w_kv_d[l][:, j * 512:(j + 1) * 512].rearrange("(kc p) c -> p kc c", p=128),
                             ("wkvb", l, j))

        if n_ptiles > 0:
            for l in range(DEPTH):
                precast_kv(l)
        for l in range(DEPTH):
            precast_layer(l)

        def w_read_keys(kind, l, blk):
            if kind == "in" and blk < 4:
                return [("winb", l, blk, s_i) for s_i in range(4)]
            return [({"in": "winb", "kv": "wkvb", "out": "woutb"}[kind], l, blk)]

        def w_issue_upto2(i):
            while wstate["issued"] <= min(i, len(wseq) - 1):
                j = wstate["issued"]
                kind, l, blk = wseq[j]
                slot = j % NW
                if kind == "in":
                    src = winb_d[l, blk]
                    dst = wring[:, slot, :].rearrange("p (k c) -> p k c", k=8)
                elif kind == "kv":
                    src = wkvb_d[l, blk] if os.environ.get("MK_KVSRC") != "1" else winb_d[l, 4 + blk]
                    dst = wring[:, slot, :].rearrange("p (k c) -> p k c", k=8)
                else:
                    src = woutb_d[l, blk]
                    dst = wring[:, slot, 0:3072].rearrange("p (k c) -> p k c", k=12)
                wq = "act" if (WQ2 and j >= len(wseq) - DEPTH * 14 and j % 2 == 1) else "sp"
                dma(wq, dst, src, w_read_keys(kind, l, blk), [("wring", slot)], ("wring", slot))
                wstate["issued"] += 1

        def w_next2(kind, l, blk):
            j = wstate["next"]
            while wseq[j] != (kind, l, blk):
                assert wstate["issued"] <= j
                wseq.pop(j)
            w_issue_upto2(j + NW - 1)
            wstate["next"] += 1
            slot = j % NW
            if kind == "out":
                view = wring[:, slot, 0:3072].rearrange("p (k c) -> p k c", k=12)
            else:
                view = wring[:, slot, :].rearrange("p (k c) -> p k c", k=8)
            return view, ("wring", slot)
        w_next = w_next2

        def rms_stats(src_fn, src_keys, nkc, T, ones_t, ones_key, eps):
            bi, bk = bank(hold=True)
            ps = psb[bi]
            for kc in range(nkc):
                si, sk = rot_xsq.next()
                S.op("act", lambda e, kc=kc, si=si: e.activation(out=xsq[:, si, :T], in_=src_fn(kc), func=AF.Square),
                     [src_keys(kc)], [sk])
                S.op("pe", lambda e, kc=kc, si=si: e.matmul(out=ps[:, :T], lhsT=ones_t[:], rhs=xsq[:, si, :T],
                                                           start=(kc == 0), stop=(kc == nkc - 1)),
                     [sk, ones_key], [bk])
            S.op("act", lambda e: e.activation(out=ps[:, :T], in_=ps[:, :T], func=AF.Ln, bias=eps), [bk], [bk])
            S.op("act", lambda e: e.activation(out=ps[:, :T], in_=ps[:, :T], func=AF.Exp, scale=-0.5), [bk], [bk])
            return bi, bk

        DEFERQ = os.environ.get("MK_DEFERQ", "1") == "1"
        PEng = ["pool"]

        def sigmoid_gate(pz, pzk, T):
            szi, szk = rot_sz.next()
            S.op("act", lambda e: e.activation(out=szb[:, szi, :T], in_=psb[pz][:, :T], func=AF.Silu), [pzk], [szk])
            return szi, szk

        def stage_view(kc):
            if kc < 4:
                return (xstage[:].rearrange("p a c -> p (a c)").rearrange("p (k t) -> p k t", k=4)[:, kc, :],
                        xkeys(0) + xkeys(1))
            return (xg[:].rearrange("p k t -> p (k t)").bitcast(F32).rearrange("p (k t) -> p k t", k=4)[:, kc - 4, :],
                    [("xg", j) for j in range(8)])

        def tile_layer(l, T, TB, gtab, gtab_key, stats=None, staged=False, prefetch=None, after_norm=None):
            NB = T // TB
            g_key = "g_all"
            skey = ("S", l)
            sbi, sbk = rot_sbf.next()
            S.op(PEng[0], lambda e, sbi=sbi: e.tensor_copy(out=Sbf[:, sbi], in_=Sst[:, l]), [skey], [sbk])
            defer_q = False
            if stats is None:
                bi, bk = rms_stats(lambda kc: xT[:, kc, :T], lambda kc: ("xT", kc), 8, T, ones_rms, "ones_rms", EPS)
            else:
                bi, bk, defer_q = stats
            for kc in range(8):
                if staged:
                    sv, skeys = stage_view(kc)
                else:
                    sv, skeys = xT[:, kc, :T], [("xT", kc)]
                S.op("dve", lambda e, kc=kc, bi=bi, sv=sv: e.scalar_tensor_tensor(
                    out=xn[:, kc, :T], in0=sv, scalar=g_all[:, l, kc:kc + 1], in1=psb[bi][:, :T],
                    op0=ALU.mult, op1=ALU.mult), skeys + [bk, g_key], [("xn", kc)])
            release(bi)
            if after_norm is not None:
                after_norm()
            if staged:
                for kc in range(8):
                    sv, skeys = stage_view(kc)
                    if kc < 4:
                        S.op("act", lambda e, kc=kc, sv=sv: e.activation(out=xT[:, kc, :T], in_=sv, func=AF.Copy),
                             skeys, [("xT", kc)])
                    else:
                        S.op(PEng[0], lambda e, kc=kc, sv=sv: e.tensor_copy(out=xT[:, kc, :T], in_=sv), skeys, [("xT", kc)])

            for name, wi in (("q", 4), ("k", 5), ("v", 6)):
                wv, wk = w_next("in", l, wi)
                pbanks = [bank() for _ in range(NB)]
                dq_ = defer_q and name == "q"
                src_t, src_n = (xg, "xg") if dq_ else (xn, "xn")
                for kc in range(8):
                    for blk in range(NB):
                        pi, pk = pbanks[blk]
                        S.op("pe", lambda e, kc=kc, blk=blk, pi=pi, wv=wv, src_t=src_t: e.matmul(
                            out=psb[pi][:TB, :], lhsT=src_t[:, kc, blk * TB:(blk + 1) * TB], rhs=wv[:, kc, :],
                            start=(kc == 0), stop=(kc == 7)), [(src_n, kc), wk], [pk])
                if name == "q" and prefetch is not None:
                    pcols = prefetch.rearrange("(kc p) t -> p kc t", p=128)
                    dma("sp", xstage[:].rearrange("p a c -> p (a c)").rearrange("p (k t) -> p k t", k=4), pcols[:, 0:4, :],
                        [], xkeys(0) + xkeys(1), ("stgA", 0))
                    dma("sp", xg[:].rearrange("p k t -> p (k t)").bitcast(F32).rearrange("p (k t) -> p k t", k=4), pcols[:, 4:8, :],
                        [], [("xg", j) for j in range(8)], ("stgB", 0))
                if dq_:
                    rbi, rbk = bank()
                    for blk in range(NB):
                        S.op("pe", lambda e, blk=blk, rbi=rbi: e.matmul(
                            out=psb[rbi][:TB, blk:blk + 1], lhsT=rs1[0:1, blk * TB:(blk + 1) * TB], rhs=ident_f[0:1, 0:1],
                            start=True, stop=True), ["rs1", "ident_f"], [rbk])
                    S.op("act", lambda e, rbi=rbi: e.activation(out=rtok[:TB, 0:NB], in_=psb[rbi][:TB, 0:NB], func=AF.Copy),
                         [rbk], ["rtok"])
                for blk in range(NB):
                    pi, pk = pbanks[blk]
                    ps = psb[pi]
                    if name in ("q", "k"):
                        At = Aq if name == "q" else Ak
                        ak = ("A" + name, blk)
                        bti, btk = rot_bt.next()
                        psv = ps[:TB, :].rearrange("p (h two d) -> p h two d", h=H, two=2)
                        Bv = Btmp[:TB, bti, :].rearrange("p (h two d) -> p h two d", h=H, two=2)
                        if dq_:
                            rsc = rtok[:TB, blk:blk + 1]
                            S.op("dve", lambda e, ps=ps, At=At, blk=blk, rsc=rsc: e.scalar_tensor_tensor(
                                out=At[:TB, blk, :].rearrange("p (h d) -> p h d", h=H),
                                in0=ps[:TB, :].rearrange("p (h d) -> p h d", h=H), scalar=rsc,
                                in1=C2[:TB, blk:blk + 1, :].broadcast_to([TB, H, 128]), op0=ALU.mult, op1=ALU.mult),
                                [pk, "rope_c", "rtok"], [ak])
                            S.op("dve", lambda e, psv=psv, Bv=Bv, blk=blk, rsc=rsc: e.scalar_tensor_tensor(
                                out=Bv[:, :, 0, :], in0=psv[:, :, 1, :], scalar=rsc,
                                in1=S2[:TB, blk:blk + 1, 0:64].broadcast_to([TB, H, 64]), op0=ALU.mult, op1=ALU.mult),
                                [pk, "rope_s", "rtok"], [(btk, 0)])
                            S.op("dve", lambda e, psv=psv, Bv=Bv, blk=blk, rsc=rsc: e.scalar_tensor_tensor(
                                out=Bv[:, :, 1, :], in0=psv[:, :, 0, :], scalar=rsc,
                                in1=S2[:TB, blk:blk + 1, 64:128].broadcast_to([TB, H, 64]), op0=ALU.mult, op1=ALU.mult),
                                [pk, "rope_s", "rtok"], [(btk, 1)])
                        else:
                            S.op("dve", lambda e, ps=ps, At=At, blk=blk: e.tensor_tensor(
                                out=At[:TB, blk, :].rearrange("p (h d) -> p h d", h=H),
                                in0=ps[:TB, :].rearrange("p (h d) -> p h d", h=H),
                                in1=C2[:TB, blk:blk + 1, :].broadcast_to([TB, H, 128]), op=ALU.mult),
                                [pk, "rope_c"], [ak])
                            S.op("dve", lambda e, psv=psv, Bv=Bv, blk=blk: e.tensor_tensor(
                                out=Bv[:, :, 0, :], in0=psv[:, :, 1, :],
                                in1=S2[:TB, blk:blk + 1, 0:64].broadcast_to([TB, H, 64]), op=ALU.mult),
                                [pk, "rope_s"], [(btk, 0)])
                            S.op("dve", lambda e, psv=psv, Bv=Bv, blk=blk: e.tensor_tensor(
                                out=Bv[:, :, 1, :], in0=psv[:, :, 0, :],
                                in1=S2[:TB, blk:blk + 1, 64:128].broadcast_to([TB, H, 64]), op=ALU.mult),
                                [pk, "rope_s"], [(btk, 1)])
                        S.op(PEng[0], lambda e, At=At, blk=blk, bti=bti: e.tensor_tensor(
                            out=At[:TB, blk, :], in0=At[:TB, blk, :], in1=Btmp[:TB, bti, :], op=ALU.add),
                            [ak, (btk, 0), (btk, 1)], [ak])
                    else:
                        S.op("act", lambda e, ps=ps, blk=blk: e.activation(out=v_sb[:TB, blk, :], in_=ps[:TB, :], func=AF.Copy),
                             [pk], [("v_sb", blk)])
                        for h in range(H):
                            S.op("act", lambda e, ps=ps, blk=blk, h=h: e.activation(
                                out=vdec[:TB, blk, h * 128:(h + 1) * 128], in_=ps[:TB, h * 128:(h + 1) * 128],
                                func=AF.Identity, scale=dkcol[:TB, h:h + 1]), [pk, "dkcol"], [("vdec", blk)])

            wv, wk = w_next("in", l, 8)
            for hc in range(H):
                pi, pk = bank()
                ps = psb[pi]
                for kc in range(8):
                    S.op("pe", lambda e, kc=kc, hc=hc, ps=ps, wv=wv: e.matmul(
                        out=ps[:, :T], lhsT=wv[:, kc, hc * 128:(hc + 1) * 128], rhs=xn[:, kc, :T],
                        start=(kc == 0), stop=(kc == 7)), [("xn", kc), wk], [pk])
                S.op("act", lambda e, ps=ps, hc=hc: e.activation(out=qxT[:, hc, :T], in_=ps[:, :T], func=AF.Copy),
                     [pk], [("qxT", hc)])

            def conv_cc(cc):
                wv, wk = w_next("in", l, cc)
                pcs = []
                for s_i in range(4):
                    pi, pk = bank()
                    pcs.append((pi, pk))
                    for kc in range(8):
                        S.op("pe", lambda e, pi=pi, kc=kc, s_i=s_i, wv=wv: e.matmul(
                            out=psb[pi][:, :T], lhsT=wv[:, kc, s_i * 128:(s_i + 1) * 128], rhs=xn[:, kc, :T],
                            start=(kc == 0), stop=(kc == 7)), [("xn", kc), wk], [pk])
                (pb, pbk), (pc, pck), (ph, phk), (pz, pzk) = pcs
                ui, uk = rot_u.next()
                s0i, s0k = rot_s0.next()
                S.op("act", lambda e, pc=pc: e.activation(out=c_sb[:, :T], in_=psb[pc][:, :T], func=AF.Copy), [pck], ["c_sb"])
                S.op("dve", lambda e, ph=ph, ui=ui: e.tensor_tensor(out=ubuf[:, ui, 2:2 + T], in0=psb[ph][:, :T],
                                                                    in1=c_sb[:, :T], op=ALU.mult), [phk, "c_sb"], [uk])
                tk = ("utail", l, cc)
                S.op(PEng[0], lambda e, ui=ui, cc=cc: e.tensor_copy(out=ubuf[:, ui, 0:2], in_=utail[:, l, cc, :]), [tk], [(uk, "t")])
                S.op(PEng[0], lambda e, ui=ui, cc=cc: e.tensor_copy(out=utail[:, l, cc, :], in_=ubuf[:, ui, T:T + 2]), [uk], [tk])
                S.op("act", lambda e, ui=ui, s0i=s0i, cc=cc: e.activation(out=s0b[:, s0i, :T], in_=ubuf[:, ui, 0:T], func=AF.Identity,
                                                                          scale=convw[:, l, 0, cc:cc + 1]), [uk, (uk, "t"), "convw"], [s0k])
                S.op("act", lambda e, ui=ui, cc=cc: e.activation(out=s1b[:, :T], in_=ubuf[:, ui, 1:T + 1], func=AF.Identity,
                                                                 scale=convw[:, l, 1, cc:cc + 1]), [uk, (uk, "t"), "convw"], ["s1b"])
                S.op("act", lambda e, ui=ui, cc=cc: e.activation(out=s2b[:, :T], in_=ubuf[:, ui, 2:T + 2], func=AF.Identity,
                                                                 scale=convw[:, l, 2, cc:cc + 1]), [uk, "convw"], ["s2b"])
                S.op(PEng[0], lambda e, s0i=s0i: e.tensor_tensor(out=s0b[:, s0i, :T], in0=s0b[:, s0i, :T], in1=s1b[:, :T], op=ALU.add),
                     [s0k, "s1b"], [s0k])
                S.op(PEng[0], lambda e, s0i=s0i: e.tensor_tensor(out=s0b[:, s0i, :T], in0=s0b[:, s0i, :T], in1=s2b[:, :T], op=ALU.add),
                     [s0k, "s2b"], [s0k])
                szi, szk = sigmoid_gate(pz, pzk, T)
                S.op("dve", lambda e, pb=pb, szi=szi: e.tensor_tensor(out=szb[:, szi, :T], in0=psb[pb][:, :T], in1=szb[:, szi, :T],
                                                                      op=ALU.mult), [pbk, szk], [szk])
                S.op(PEng[0], lambda e, s0i=s0i, szi=szi, cc=cc: e.tensor_tensor(out=yT[:, cc, :T], in0=s0b[:, s0i, :T],
                                                                                in1=szb[:, szi, :T], op=ALU.mult),
                     [s0k, szk], [("yT", cc)])

            for name, At, dst, dtab, dkey in (("q", Aq, qT, dqb, "dqb"), ("k", Ak, kT, dkb, "dkb")):
                for h in range(H):
                    pi, pk = bank()
                    ps = psb[pi]
                    for blk in range(NB):
                        S.op("pe", lambda e, ps=ps, At=At, blk=blk, h=h: e.matmul(
                            out=ps[:, blk * TB:(blk + 1) * TB], lhsT=At[:TB, blk, h * 128:(h + 1) * 128],
                            rhs=ident_b[:TB, :TB], start=True, stop=True), [("A" + name, blk), "ident_b"], [pk])
                    S.op("dve", lambda e, ps=ps, dst=dst, dtab=dtab, h=h: e.tensor_tensor(
                        out=dst[:, h, :T].rearrange("p (n t) -> p n t", t=TB),
                        in0=ps[:, :T].rearrange("p (n t) -> p n t", t=TB),
                        in1=dtab[:, h:h + 1, :TB].broadcast_to([128, NB, TB]), op=ALU.mult),
                        [pk, dkey], [(name + "T", h)])

            def mem_scores(h):
                pti, ptk = rot_pt.next()
                for mc in range(2):
                    pi, pk = bank()
                    ps = psb[pi]
                    S.op("pe", lambda e, ps=ps, h=h, mc=mc: e.matmul(
                        out=ps[:, :T], lhsT=KT[:, l, h, mc * 128:(mc + 1) * 128], rhs=qxT[:, h, :T],
                        start=True, stop=True), [("KT", l), ("qxT", h)], [pk])
                    S.op("act", lambda e, ps=ps, pti=pti, mc=mc: e.activation(
                        out=PT[:, pti, mc, :T], in_=ps[:, :T], func=AF.Exp, scale=float(HD) ** -0.5), [pk], [(ptk, mc)])
                return pti, ptk

            def mem_pv(h, pti, ptk):
                oi, ok = bank()
                di, dk_ = bank()
                for mc in range(2):
                    S.op("pe", lambda e, di=di, pti=pti, mc=mc: e.matmul(
                        out=psb[di][:, :T], lhsT=ones_1[:], rhs=PT[:, pti, mc, :T],
                        start=(mc == 0), stop=(mc == 1)), ["ones_1", (ptk, mc)], [dk_])
                for mc in range(2):
                    S.op("pe", lambda e, oi=oi, pti=pti, mc=mc, h=h: e.matmul(
                        out=psb[oi][:, :T], lhsT=Vm[:, l, mc, h * 128:(h + 1) * 128], rhs=PT[:, pti, mc, :T],
                        start=(mc == 0), stop=(mc == 1)), [("Vm", l), (ptk, mc)], [ok])
                oxn = scrA[:, h * 512:h * 512 + T]
                ri, rk = rot_rs.next()
                S.op("act", lambda e, di=di: e.activation(out=psb[di][:, :T], in_=psb[di][:, :T], func=AF.Ln), [dk_], [dk_])
                S.op("act", lambda e, di=di, ri=ri: e.activation(out=rsb[:, ri, :T], in_=psb[di][:, :T], func=AF.Exp, scale=-1.0),
                     [dk_], [rk])
                S.op("dve", lambda e, oi=oi, ri=ri, oxn=oxn: e.tensor_tensor(out=oxn, in0=psb[oi][:, :T], in1=rsb[:, ri, :T],
                                                                              op=ALU.mult), [ok, rk], [("scrA", h)])

            mp = [None] * H
            mp[0] = mem_scores(0)
            mp[1] = mem_scores(1)
            mem_pv(0, *mp[0])
            mp[2] = mem_scores(2)
            mem_pv(1, *mp[1])
            mp[3] = mem_scores(3)
            mem_pv(2, *mp[2])
            mem_pv(3, *mp[3])

            wz = {}

            def z_gate(which, h):
                wi, base = (9, 8) if which == "x" else (7, 4)
                if which not in wz:
                    wz[which] = w_next("in", l, wi)
                wv, wk = wz[which]
                pi, pk = bank()
                for kc in range(8):
                    S.op("pe", lambda e, pi=pi, kc=kc, h=h, wv=wv: e.matmul(
                        out=psb[pi][:, :T], lhsT=wv[:, kc, h * 128:(h + 1) * 128], rhs=xn[:, kc, :T],
                        start=(kc == 0), stop=(kc == 7)), [("xn", kc), wk], [pk])
                szi, szk = sigmoid_gate(pi, pk, T)
                if which == "x":
                    src_ap, src_key = scrA[:, h * 512:h * 512 + T], ("scrA", h)
                else:
                    src_ap, src_key = orn[:, h, :T], ("orn", h)
                S.op(PEng[0], lambda e, szi=szi, h=h, base=base, src_ap=src_ap: e.tensor_tensor(
                    out=yT[:, base + h, :T], in0=src_ap, in1=szb[:, szi, :T], op=ALU.mult),
                    [szk, src_key], [("yT", base + h)])

            obanks = [bank(hold=True) for _ in range(H)]
            skey = ("S", l)
            zx_left = list(range(H))
            per = (H + NB - 1) // NB
            def ret_scores(blk):
                sci, sck = bank()
                for h in range(H):
                    S.op("pe", lambda e, sci=sci, h=h, blk=blk: e.matmul(
                        out=psb[sci][:TB, h * TB:(h + 1) * TB], lhsT=kT[:, h, blk * TB:(blk + 1) * TB],
                        rhs=qT[:, h, blk * TB:(blk + 1) * TB], start=True, stop=True),
                        [("kT", h), ("qT", h)], [sck])
                mi, mk_ = rot_scm.next()
                S.op("dve", lambda e, sci=sci, mi=mi: e.tensor_tensor(
                    out=scm[:TB, mi, :, :TB], in0=psb[sci][:TB, :H * TB].rearrange("p (h t) -> p h t", h=H),
                    in1=mask_b[:TB, 0:TB].unsqueeze(1).broadcast_to([TB, H, TB]), op=ALU.mult),
                    [sck, "mask_b"], [mk_])
                ki, kk = bank()
                for h in range(H):
                    S.op("pe", lambda e, ki=ki, h=h, blk=blk: e.matmul(
                        out=psb[ki][:, h * 128:(h + 1) * 128], lhsT=Ak[:TB, blk, h * 128:(h + 1) * 128],
                        rhs=vdec[:TB, blk, h * 128:(h + 1) * 128], start=True, stop=True),
                        [("Ak", blk), ("vdec", blk)], [kk])
                return mi, mk_, ki, kk

            def ret_out(blk, mi, mk_, ki, kk, sbi, sbk):
                for h in range(H):
                    oi, ok = obanks[h]
                    S.op("pe", lambda e, oi=oi, mi=mi, h=h, blk=blk: e.matmul(
                        out=psb[oi][:, blk * TB:(blk + 1) * TB], lhsT=v_sb[:TB, blk, h * 128:(h + 1) * 128],
                        rhs=scm[:TB, mi, h, :TB], start=True, stop=False), [("v_sb", blk), mk_], [ok])
                    S.op("pe", lambda e, oi=oi, sbi=sbi, h=h, blk=blk: e.matmul(
                        out=psb[oi][:, blk * TB:(blk + 1) * TB], lhsT=Sbf[:, sbi, h, :],
                        rhs=qT[:, h, blk * TB:(blk + 1) * TB], start=False, stop=True), [sbk, ("qT", h)], [ok])

            def ret_state(blk, ki, kk):
                Sl = Sst[:, l].rearrange("p h e -> p (h e)")
                S.op("dve", lambda e, ki=ki, Sl=Sl: e.tensor_tensor(out=Sl, in0=psb[ki][:, :], in1=Sl, op=ALU.add),
                     [kk, skey], [skey])
                nsbi, nsbk = None, None
                if blk < NB - 1:
                    nsbi, nsbk = rot_sbf.next()
                    S.op("dve", lambda e, Sl=Sl, gtab=gtab, nsbi=nsbi: e.tensor_tensor(
                        out=Sbf[:, nsbi].rearrange("p h e -> p (h e)"), in0=Sl,
                        in1=gtab[:].rearrange("p h e -> p (h e)"), op=ALU.mult), [skey, gtab_key], [nsbk])
                S.op(PEng[0], lambda e, Sl=Sl, gtab=gtab: e.tensor_tensor(
                    out=Sl, in0=Sl, in1=gtab[:].rearrange("p h e -> p (h e)"), op=ALU.mult), [skey, gtab_key], [skey])
                return nsbi, nsbk

            cur = ret_scores(0)
            for blk in range(NB):
                nxt = ret_scores(blk + 1) if blk + 1 < NB else None
                ret_out(blk, cur[0], cur[1], cur[2], cur[3], sbi, sbk)
                sbi, sbk = ret_state(blk, cur[2], cur[3])
                cur = nxt

            sq = []
            for h in range(H):
                oi, ok = obanks[h]
                qi, qk = rot_osq.next()
                S.op("act", lambda e, oi=oi, qi=qi: e.activation(out=osq[:, qi, :T], in_=psb[oi][:, :T], func=AF.Square),
                     [ok], [qk])
                sq.append((qi, qk))
            for h in range(H):
                oi, ok = obanks[h]
                qi, qk = sq[h]
                mi2, mk2 = bank()
                S.op("pe", lambda e, mi2=mi2, qi=qi: e.matmul(out=psb[mi2][:, :T], lhsT=ones_h[:], rhs=osq[:, qi, :T],
                                                             start=True, stop=True), ["ones_h", qk], [mk2])
                ri, rk = rot_rs.next()
                S.op("act", lambda e, mi2=mi2: e.activation(out=psb[mi2][:, :T], in_=psb[mi2][:, :T], func=AF.Ln, bias=EPS),
                     [mk2], [mk2])
                S.op("act", lambda e, mi2=mi2, ri=ri: e.activation(out=rsb[:, ri, :T], in_=psb[mi2][:, :T], func=AF.Exp, scale=-0.5),
                     [mk2], [rk])
                S.op("dve", lambda e, oi=oi, ri=ri, h=h: e.tensor_tensor(out=orn[:, h, :T], in0=psb[oi][:, :T],
                                                                         in1=rsb[:, ri, :T], op=ALU.mult),
                     [ok, rk], [("orn", h)])
                release(oi)
            for cc in range(4):
                conv_cc(cc)
            for h in range(H):
                z_gate("x", h)
            for h in range(H):
                z_gate("r", h)

            pre_stats = None
            if prefetch is not None:
                pre_stats = rms_stats(lambda kc: stage_view(kc)[0], lambda kc: stage_view(kc)[1][0], 8, T, ones_rms, "ones_rms", EPS)

            nbi, nbk = bank(hold=True)
            pend = []
            nxt_defer = (l < DEPTH - 1) and DEFERQ

            def stats_part(oc):
                si, sk = rot_xsq.next()
                S.op("act", lambda e, oc=oc, si=si: e.activation(out=xsq[:, si, :T], in_=xT[:, oc, :T], func=AF.Square),
                     [("xT", oc)], [sk])
                S.op("pe", lambda e, oc=oc, si=si: e.matmul(out=psb[nbi][:, :T], lhsT=ones_rms[:], rhs=xsq[:, si, :T],
                                                           start=(oc == 0), stop=(oc == 7)), [sk, "ones_rms"], [nbk])

            for j in range(4):
                wv, wk = w_next("out", l, j)
                for half in range(2):
                    oc = j * 2 + half
                    pi, pk = bank()
                    for kc in range(12):
                        S.op("pe", lambda e, pi=pi, kc=kc, half=half, wv=wv: e.matmul(
                            out=psb[pi][:, :T], lhsT=wv[:, kc, half * 128:(half + 1) * 128], rhs=yT[:, kc, :T],
                            start=(kc == 0), stop=(kc == 11)), [("yT", kc), wk], [pk])
                    S.op("dve", lambda e, pi=pi, oc=oc: e.tensor_tensor(out=xT[:, oc, :T], in0=psb[pi][:, :T],
                                                                        in1=xT[:, oc, :T], op=ALU.add),
                         [pk, ("xT", oc)], [("xT", oc)])
                    if pend:
                        stats_part(pend.pop(0))
                    pend.append(oc)
                    if nxt_defer:
                        S.op("act", lambda e, oc=oc: e.activation(out=xg[:, oc, :T], in_=xT[:, oc, :T], func=AF.Identity,
                                                                  scale=g_all[:, l + 1, oc:oc + 1]), [("xT", oc), g_key], [("xg", oc)])
            while pend:
                stats_part(pend.pop(0))
            S.op("act", lambda e: e.activation(out=psb[nbi][:, :T], in_=psb[nbi][:, :T], func=AF.Ln, bias=EPS), [nbk], [nbk])
            if nxt_defer:
                S.op("act", lambda e: e.activation(out=rs1[0:1, :T], in_=psb[nbi][0:1, :T], func=AF.Exp, scale=-0.5), [nbk], ["rs1"])
            S.op("act", lambda e: e.activation(out=psb[nbi][:, :T], in_=psb[nbi][:, :T], func=AF.Exp, scale=-0.5), [nbk], [nbk])
            if prefetch is not None:
                return (nbi, nbk, nxt_defer), pre_stats
            return nbi, nbk, nxt_defer

        def load_x_tile(src_cols, T, TB):
            for half in range(2):
                dma("sp", xT[:, half * 4:(half + 1) * 4, :T],
                    src_cols.rearrange("(kc p) t -> p kc t", p=128)[:, half * 4:(half + 1) * 4, :],
                    [], [("xT", half * 4 + j) for j in range(4)], ("xTld", half))

        def final_store(dst_cols, T, TB, outkey, stats=None):
            if stats is None:
                bi, bk = rms_stats(lambda kc: xT[:, kc, :T], lambda kc: ("xT", kc), 8, T, ones_rms, "ones_rms", EPS)
            else:
                bi, bk = stats[0], stats[1]
            ystA = scrA[:].rearrange("p (k t) -> p k t", k=4)
            for kc in range(8):
                dst = ystA[:, kc, :T] if kc < 4 else orn[:, kc - 4, :T]
                key = ("scrA", kc) if kc < 4 else ("orn", kc - 4)
                S.op("dve", lambda e, kc=kc, bi=bi, dst=dst: e.scalar_tensor_tensor(
                    out=dst, in0=xT[:, kc, :T], scalar=gfin[:, kc:kc + 1], in1=psb[bi][:, :T],
                    op0=ALU.mult, op1=ALU.mult), [("xT", kc), bk, "gfin"], [key])
            release(bi)
            dv = dst_cols.rearrange("(kc p) t -> p kc t", p=128)
            toks = [dma("act", dv[:, 0:4, :], ystA[:, :, :T], [("scrA", h) for h in range(4)], [(outkey, 0)], ("yst", 0)),
                    dma("act", dv[:, 4:8, :], orn[:, :, :T], [("orn", h) for h in range(4)], [(outkey, 1)], ("yst", 1))]
            return toks

        out_toks = []

        STAGE = int(os.environ.get("MK_STAGE", "9"))
        def prep_kT(l):
            for hp in range(2):
                pi, pk = bank()
                for hh in range(2):
                    h = hp * 2 + hh
                    for mc in range(2):
                        S.op("pe", lambda e, pi=pi, hh=hh, h=h, mc=mc: e.matmul(
                            out=psb[pi][:, hh * 256 + mc * 128: hh * 256 + (mc + 1) * 128],
                            lhsT=Kb[:, mc, h * 128:(h + 1) * 128], rhs=ident_b[:], start=True, stop=True),
                            [("Kb", mc), "ident_b"], [pk])
                S.op("act", lambda e, pi=pi, hp=hp: e.activation(
                    out=KT[:, l, hp * 2:hp * 2 + 2, :], in_=psb[pi][:, :].rearrange("p (h m) -> p h m", h=2), func=AF.Copy),
                    [pk], [("KT", l)])

        if n_ptiles > 0 and STAGE >= 4:
            for l in range(DEPTH):
                S.op("dve", lambda e, l=l: e.memset(Sst[:, l].rearrange("p h e -> p (h e)"), 0.0), [], [("S", l)])
            S.op("dve", lambda e: e.memset(utail[:].rearrange("p l c t -> p (l c t)"), 0.0), [],
                 [("utail", l, cc) for l in range(DEPTH) for cc in range(4)])
            SUB = int(os.environ.get("MK_SUB", "9"))
            memT = scrA[:].rearrange("p (k m) -> p k m", k=8)
            scr_keys = [("scrA", h) for h in range(H)]
            if SUB >= 2:
                dma("sp", memT, mem_d.rearrange("(kc p) m -> p kc m", p=128), [], scr_keys, ("memld", 0))
            if SUB >= 3:
                bi, bk = rms_stats(lambda kc: memT[:, kc, :], lambda kc: ("scrA", 0), 8, NMEM, ones_rms, "ones_rms", EPS)
            for l in range(DEPTH if SUB >= 3 else 0):
                for kc in range(8):
                    S.op("dve", lambda e, kc=kc, bi=bi, l=l: e.scalar_tensor_tensor(
                        out=memn[:, kc, :], in0=memT[:, kc, :], scalar=gmem[:, l, kc:kc + 1], in1=psb[bi][:, :NMEM],
                        op0=ALU.mult, op1=ALU.mult), scr_keys + [bk, "gmem"], [("memn", kc)])
                for j, (odst, name) in enumerate(((nmk_d, "nmk"), (nmv_d, "nmv")) if SUB >= 4 else ()):
                    wv, wk = w_next("kv", l, j)
                    for mc in range(2):
                        pi, pk = bank()
                        for kc in range(8):
                            S.op("pe", lambda e, pi=pi, kc=kc, mc=mc, wv=wv: e.matmul(
                                out=psb[pi][:, :], lhsT=memn[:, kc, mc * 128:(mc + 1) * 128], rhs=wv[:, kc, :],
                                start=(kc == 0), stop=(kc == 7)), [("memn", kc), wk], [pk])
                        ki, kk = rot_kvs.next()
                        S.op("act", lambda e, pi=pi, ki=ki: e.activation(out=kvstage[:, ki, :], in_=psb[pi][:, :], func=AF.Copy),
                             [pk], [kk])
                        if os.environ.get("MK_NOKVST") != "1":
                            out_toks.append(dma(os.environ.get("MK_KVQ", "act"), odst[l][mc * 128:(mc + 1) * 128, :], kvstage[:, ki, :], [kk],
                                                [(name, l, mc)], kk))
                        if j == 0:
                            S.op("dve", lambda e, ki=ki, mc=mc: e.tensor_copy(out=Kb[:, mc, :], in_=kvstage[:, ki, :]), [kk], [("Kb", mc)])
                        else:
                            S.op("dve", lambda e, ki=ki, mc=mc, l=l: e.tensor_copy(out=Vm[:, l, mc, :], in_=kvstage[:, ki, :]),
                                 [kk], [("Vm", l)])
                    if j == 0:
                        prep_kT(l)
            if SUB >= 3:
                release(bi)
            S.op("dve", lambda e: e.memset(c_sb[0:1, 0:1], 0.0), [],
                 [("memn", kc) for kc in range(8)] + [("Kb", mc) for mc in range(2)] + [("kvstage", 0)]
                 + [("yT", j) for j in range(12)] + ["c_sb"])

            nxt_pre = None
            pending_store = None
            PREFETCH = os.environ.get("MK_PREFETCH", "1") == "1"
            for t in range(n_ptiles if STAGE >= 5 else 0):
                dma("sp", C2[:, :, :], c2p_d[:, t * 4:(t + 1) * 4, :], [], ["rope_c"], "C2")
                dma("sp", S2[:, :, :], s2p_d[:, t * 4:(t + 1) * 4, :], [], ["rope_s"], "S2")
                if nxt_pre is None:
                    load_x_tile(xp_d[:, t * TP:(t + 1) * TP], TP, 128)
                st_ = None
                PEng[0] = "dve" if t == 0 else "pool"
                for l in range(DEPTH):
                    if l == 0 and nxt_pre is not None:
                        st_ = tile_layer(l, TP, 128, gtab_p, "gtab_p", (nxt_pre[0], nxt_pre[1], False), staged=True,
                                         after_norm=pending_store)
                        pending_store = None
                        nxt_pre = None
                    elif l == DEPTH - 1 and t + 1 < n_ptiles and PREFETCH:
                        st_, nxt_pre = tile_layer(l, TP, 128, gtab_p, "gtab_p", st_, prefetch=xp_d[:, (t + 1) * TP:(t + 2) * TP])
                    else:
                        st_ = tile_layer(l, TP, 128, gtab_p, "gtab_p", st_)
                if nxt_pre is not None:
                    def pending_store(t=t, st_=st_):
                        out_toks.extend(final_store(yp_d[:, t * TP:(t + 1) * TP], TP, 128, ("yp", t), st_))
                else:
                    out_toks += final_store(yp_d[:, t * TP:(t + 1) * TP], TP, 128, ("yp", t), st_)
            out_toks.append(dma("act", nrp_d.rearrange("l h d e -> d (l h) e"), Sst[:].rearrange("p l h e -> p (l h) e"),
                                [("S", l) for l in range(DEPTH)], ["nrp"], "Sst"))
            out_toks.append(dma("act", ncp_d, utail[:], [("utail", l, cc) for l in range(DEPTH) for cc in range(4)],
                                ["ncp"], "utail"))

        dma("sp", Sst[:].rearrange("p l h e -> p (l h) e"), sret_d.rearrange("l h d e -> d (l h) e"),
            [], [("S", l) for l in range(DEPTH)], "Sst")
        dma("sp", utail[:], sconv_d, [], [("utail", l, cc) for l in range(DEPTH) for cc in range(4)], "utail")
        dma("sp", C2[:32, 0:1, :], c2s_d, [], ["rope_c"], "C2")
        dma("sp", S2[:32, 0:1, :], s2s_d, [], ["rope_s"], "S2")

        for l in range(DEPTH if STAGE >= 1 else 0):
            dma("sp", xstage[:, 0, :].rearrange("p (mc c) -> p mc c", mc=2), cmk_d[l].rearrange("(mc p) c -> p mc c", p=128),
                [], xkeys(0), ("xstage", 0))
            S.op("dve", lambda e: e.tensor_copy(out=Kb[:].rearrange("p mc c -> p (mc c)"), in_=xstage[:, 0, :]),
                 xkeys(0), [("Kb", 0), ("Kb", 1)])
            prep_kT(l)
            dma("sp", xstage[:, 1, :].rearrange("p (mc c) -> p mc c", mc=2), cmv_d[l].rearrange("(mc p) c -> p mc c", p=128),
                [], xkeys(1), ("xstage", 1))
            S.op("dve", lambda e, l=l: e.tensor_copy(out=Vm[:, l].rearrange("p mc c -> p (mc c)"), in_=xstage[:, 1, :]),
                 xkeys(1), [("Vm", l)])

        if STAGE >= 2:
            load_x_tile(xs_d[:, :], 32, 32)
        NLS = int(os.environ.get("MK_NLS", DEPTH))
        st_ = None
        PEng[0] = "pool"
        if STAGE >= 3:
            for l in range(NLS):
                st_ = tile_layer(l, 32, 32, gtab_s, "gtab_s", st_)
        if STAGE >= 2:
            out_toks += final_store(ys_d[:, :], 32, 32, "ys", st_)
        out_toks.append(dma("act", nrs_d.rearrange("l h d e -> d (l h) e"), Sst[:].rearrange("p l h e -> p (l h) e"),
                            [("S", l) for l in range(DEPTH)], ["nrs"], "Sst"))
        out_toks.append(dma("act", ncs_d, utail[:], [("utail", l, cc) for l in range(DEPTH) for cc in range(4)],
                            ["ncs"], "utail"))

        S.final_wait("act", out_toks)
        S.final_wait("sp", [S.last_writer[("wring", s)] for s in range(NW) if ("wring", s) in S.last_writer])
        S.emit(st)
    return nc


def _tables():
    f32 = np.float32
    freqs = (np.float32(10000.0) ** (-np.arange(0, HD, 2, dtype=np.float32) / np.float32(HD))).astype(f32)

    def cs(pos):
        ang = (pos.astype(f32)[:, None] * freqs[None, :]).astype(f32).astype(np.float64)
        c, s = np.cos(ang), np.sin(ang)
        c2 = np.concatenate([c, c], axis=1).astype(f32)
        s2 = np.concatenate([-s, s], axis=1).astype(f32)
        return c2, s2

    c2p, s2p = cs(np.arange(SEQ))
    c2p = np.ascontiguousarray(c2p.reshape(SEQ // 128, 128, 128).transpose(1, 0, 2))
    s2p = np.ascontiguousarray(s2p.reshape(SEQ // 128, 128, 128).transpose(1, 0, 2))
    c2s, s2s = cs(PAST + np.arange(32))
    c2s = np.ascontiguousarray(c2s.reshape(32, 1, 128))
    s2s = np.ascontiguousarray(s2s.reshape(32, 1, 128))
    gam = 1.0 - np.power(2.0, -5.0 - np.arange(H, dtype=np.float64))
    lg = np.log(gam)
    i = np.arange(128, dtype=np.float64)
    dq = np.exp((i[None, :] + 1) * lg[:, None])
    dk = np.exp(-(i[None, :] + 1) * lg[:, None]) * (HD ** -0.5)
    dqb = np.ascontiguousarray(np.broadcast_to(dq[None], (128, H, 128))).astype(f32)
    dkb = np.ascontiguousarray(np.broadcast_to(dk[None], (128, H, 128))).astype(f32)
    dkcol = np.ascontiguousarray(dk.T).astype(f32)
    gtp = np.ascontiguousarray(np.broadcast_to(np.exp(128 * lg)[None, :, None], (128, H, 128))).astype(f32)
    gts = np.ascontiguousarray(np.broadcast_to(np.exp(32 * lg)[None, :, None], (128, H, 128))).astype(f32)
    mask = (np.arange(128)[None, :] >= np.arange(128)[:, None]).astype(f32)
    return dict(c2p=c2p, s2p=s2p, c2s=c2s, s2s=s2s, dqb=dqb, dkb=dkb, dkcol=dkcol, gtab_p=gtp, gtab_s=gts,
                mask=mask, ident=np.eye(128, dtype=f32))


_NC_CACHE = {}


def kernel(x_prompt, x_sample, mem_prompt, state_conv, state_ret, cache_mem_k, cache_mem_v,
           norm_g, w_in, conv_w, mem_norm_g, w_mem_kv, w_out, final_norm_g):
    n_ptiles = int(os.environ.get("MK_NPT", SEQ // TP))
    f32 = np.float32
    A = lambda a: np.ascontiguousarray(np.asarray(a, dtype=f32))
    x_prompt, x_sample, mem_prompt = A(x_prompt), A(x_sample), A(mem_prompt)
    state_conv, state_ret = A(state_conv), A(state_ret)
    cache_mem_k, cache_mem_v = A(cache_mem_k), A(cache_mem_v)
    w_in, w_mem_kv, w_out = A(w_in), A(w_mem_kv), A(w_out)
    tabs = _tables()
    g_all = np.ascontiguousarray(A(norm_g).reshape(DEPTH, 8, 128).transpose(2, 0, 1))
    gmem = np.ascontiguousarray(A(mem_norm_g).reshape(DEPTH, 8, 128).transpose(2, 0, 1))
    gfin = np.ascontiguousarray(A(final_norm_g).reshape(8, 128).transpose(1, 0))
    convw = np.ascontiguousarray(A(conv_w).reshape(DEPTH, 3, 4, 128).transpose(3, 0, 1, 2))
    zeros_p = np.zeros((D, SEQ), f32)
    PC = [0, 1, 4, 5]
    in_maps = []
    for c in range(NCORES):
        m = dict(tabs)
        m["xp"] = np.ascontiguousarray(x_prompt[PC.index(c)].T) if c in PC else zeros_p
        m["xs"] = np.ascontiguousarray(x_sample[c].T)
        m["mem"] = np.ascontiguousarray(mem_prompt[PC.index(c) if c in PC else 0].T)
        m["sconv"] = np.ascontiguousarray(state_conv[:, c].reshape(DEPTH, 2, 4, 128).transpose(3, 0, 2, 1))
        m["sret"] = np.ascontiguousarray(state_ret[:, c])
        m["cmk"] = np.ascontiguousarray(cache_mem_k[:, c].reshape(DEPTH, NMEM, 512))
        m["cmv"] = np.ascontiguousarray(cache_mem_v[:, c].reshape(DEPTH, NMEM, 512))
        m["w_in"], m["w_kv"], m["w_out"] = w_in, w_mem_kv, w_out
        m["g_all"], m["gmem"], m["gfin"], m["convw"] = g_all, gmem, gfin, convw
        in_maps.append(m)
    if n_ptiles not in _NC_CACHE:
        _NC_CACHE[n_ptiles] = build_program(n_ptiles)
    nc = _NC_CACHE[n_ptiles]
    res = run_bass_kernel_spmd(nc, in_maps, core_ids=list(range(NCORES)))
    R = res.results
    unconv = lambda a: np.ascontiguousarray(np.asarray(a).transpose(1, 3, 2, 0).reshape(DEPTH, 2, 512))
    y_prompt = np.stack([np.asarray(R[c]["yp"]).T for c in PC]).astype(f32)
    y_sample = np.stack([np.asarray(R[c]["ys"]).T for c in range(8)]).astype(f32)
    ncp = np.stack([unconv(R[c]["ncp"]) for c in PC], axis=1).astype(f32)
    nrp = np.stack([np.asarray(R[c]["nrp"]) for c in PC], axis=1).astype(f32)
    nmk = np.stack([np.asarray(R[c]["nmk"]).reshape(DEPTH, NMEM, H, HD) for c in PC], axis=1).astype(f32)
    nmv = np.stack([np.asarray(R[c]["nmv"]).reshape(DEPTH, NMEM, H, HD) for c in PC], axis=1).astype(f32)
    ncs = np.stack([unconv(R[c]["ncs"]) for c in range(8)], axis=1).astype(f32)
    nrs = np.stack([np.asarray(R[c]["nrs"]) for c in range(8)], axis=1).astype(f32)
    return (y_prompt, y_sample, ncp, nrp, nmk, nmv, ncs, nrs)
```
